# Optimizing a Trainium2 kernel written in Bass

```python
import jax, jax.numpy as jnp
from jax import lax
import numpy as np

D_MODEL = 2048
BATCH = 1
SEQ = 8192
DEPTH = 2

GRID_W = 64
CTX_LEN = 256
HEAD_DIM = 128
D_FF = 5632
N_MOD = 9
EPS = 1e-6
ROPE_BASE = 10000.0
NEG_INF = -1e30

ATTN_HEADS = 8
ATTN_KV_HEADS = 2
ATTN_BLOCK = 128
ATTN_WINDOW = 128
CONV_CH = 1024
CONV_W = 3
EVEN_IN = ATTN_HEADS * HEAD_DIM + 2 * ATTN_KV_HEADS * HEAD_DIM + 3 * CONV_CH
EVEN_OUT = ATTN_HEADS * HEAD_DIM + CONV_CH

RET_HEADS = 4
RET_DK = 128
RET_DV = 256
RET_CHUNK = 128
NA_HEADS = 8
NA_ROWS = 8
NA_COLS = 16
NA_BLOCK = 128
ODD_IN = 2 * RET_HEADS * RET_DK + 2 * RET_HEADS * RET_DV + 3 * NA_HEADS * HEAD_DIM
ODD_OUT = RET_HEADS * RET_DV + NA_HEADS * HEAD_DIM

N_EVEN = (DEPTH + 1) // 2
N_ODD = DEPTH // 2

kernel_name = 'hybrid_dit_ctx_prefix_trunk'


def rmsnorm(x, w):
    x32 = x.astype(jnp.float32)
    y = x32 * lax.rsqrt(jnp.mean(x32 * x32, axis=-1, keepdims=True) + EPS)
    return (y * w.astype(jnp.float32)).astype(x.dtype)


def modulate(h, shift, scale):
    return h * (1 + scale) + shift


def swiglu(h, wi, wo):
    g, u = jnp.split(h @ wi, 2, axis=-1)
    return (jax.nn.silu(g) * u) @ wo


def ffn_half(h, norm_w, shift, scale, gate, wi, wo):
    return h + 0.5 * gate * swiglu(modulate(rmsnorm(h, norm_w), shift, scale), wi, wo)


def split_cols(p, sizes):
    return jnp.split(p, [int(s) for s in np.cumsum(sizes)[:-1]], axis=-1)


def axial_rope(length, dim):
    t = jnp.arange(length)
    rows = (t // GRID_W).astype(jnp.float32)
    cols = (t % GRID_W).astype(jnp.float32)
    half = dim // 2
    inv = ROPE_BASE ** (-jnp.arange(0, half, 2, dtype=jnp.float32) / half)
    ang = jnp.concatenate([rows[:, None] * inv, cols[:, None] * inv], axis=-1)
    return jnp.cos(ang), jnp.sin(ang)


def apply_rope(x, cos, sin):
    B, L, H, d = x.shape
    q = d // 4
    xr = x.astype(jnp.float32).reshape(B, L, H, 2, 2, q)
    x1, x2 = xr[..., 0, :], xr[..., 1, :]
    cs = cos.reshape(L, 2, q)[None, :, None]
    sn = sin.reshape(L, 2, q)[None, :, None]
    out = jnp.stack([x1 * cs - x2 * sn, x1 * sn + x2 * cs], axis=-2)
    return out.reshape(B, L, H, d).astype(x.dtype)


def context_attention(q, k, v, sink=None):
    B, L, H, d = q.shape
    G = k.shape[2]
    R = H // G
    s = jnp.einsum('bqgrd,bkgd->bgrqk', q.reshape(B, L, G, R, d), k).astype(jnp.float32) * (d ** -0.5)
    if sink is not None:
        sk = jnp.broadcast_to(sink.astype(jnp.float32).reshape(1, G, R, 1, 1), s.shape[:-1] + (1,))
        s = jnp.concatenate([s, sk], axis=-1)
    p = jax.nn.softmax(s, axis=-1)[..., :L].astype(v.dtype)
    return jnp.einsum('bgrqk,bkgd->bqgrd', p, v).reshape(B, L, H * d)


def windowed_gqa_latent(q, k, v, kc, vc, sink):
    B, S, H, d = q.shape
    G = k.shape[2]
    R = H // G
    T = ATTN_BLOCK
    nb = S // T
    scale = d ** -0.5
    qb = q.reshape(B, nb, T, G, R, d)

    def band(a):
        ap = jnp.pad(a, ((0, 0), (T, T), (0, 0), (0, 0))).reshape(B, nb + 2, T, G, d)
        return jnp.concatenate([ap[:, :-2], ap[:, 1:-1], ap[:, 2:]], axis=2)

    kb, vb = band(k), band(v)
    qi = jnp.arange(T)
    ki = jnp.arange(3 * T)
    rel = ki[None, :] - T - qi[:, None]
    kpos = jnp.arange(nb)[:, None] * T + ki[None, :] - T
    mask = (jnp.abs(rel) <= ATTN_WINDOW)[None] & ((kpos >= 0) & (kpos < S))[:, None, :]
    s_loc = jnp.einsum('bnqgrd,bnkgd->bgrnqk', qb, kb).astype(jnp.float32) * scale
    s_loc = jnp.where(mask, s_loc, NEG_INF)
    s_ctx = jnp.einsum('bnqgrd,bcgd->bgrnqc', qb, kc).astype(jnp.float32) * scale
    s_sink = jnp.broadcast_to(sink.astype(jnp.float32).reshape(1, G, R, 1, 1, 1), s_loc.shape[:-1] + (1,))
    p = jax.nn.softmax(jnp.concatenate([s_loc, s_ctx, s_sink], axis=-1), axis=-1).astype(v.dtype)
    p_loc = p[..., :3 * T]
    p_ctx = p[..., 3 * T:3 * T + kc.shape[1]]
    o = jnp.einsum('bgrnqk,bnkgd->bnqgrd', p_loc, vb) + jnp.einsum('bgrnqc,bcgd->bnqgrd', p_ctx, vc)
    return o.reshape(B, S, H * d)


def short_conv(z, w, b):
    L = z.shape[1]
    zp = jnp.pad(z, ((0, 0), (1, 1), (0, 0)))
    return zp[:, :L] * w[0] + zp[:, 1:L + 1] * w[1] + zp[:, 2:] * w[2] + b


def project_even(h, w_in):
    B, L, _ = h.shape
    q, k, v, bg, cg, u = split_cols(h @ w_in, [ATTN_HEADS * HEAD_DIM, ATTN_KV_HEADS * HEAD_DIM,
                                              ATTN_KV_HEADS * HEAD_DIM, CONV_CH, CONV_CH, CONV_CH])
    return (q.reshape(B, L, ATTN_HEADS, HEAD_DIM), k.reshape(B, L, ATTN_KV_HEADS, HEAD_DIM),
            v.reshape(B, L, ATTN_KV_HEADS, HEAD_DIM), bg, cg, u)


def mixer_attn_conv(h_c, h_x, w_in, w_out, sink, conv_w, conv_b, cos, sin, ctx_out):
    qx, kx, vx, bx, cx, ux = project_even(h_x, w_in)
    qc, kc, vc, bc, cc, uc = project_even(h_c, w_in)
    qx = apply_rope(qx, cos, sin)
    kx = apply_rope(kx, cos, sin)
    att_x = windowed_gqa_latent(qx, kx, vx, kc, vc, sink)
    conv_x = bx * short_conv(cx * ux, conv_w, conv_b)
    y_x = jnp.concatenate([att_x, conv_x], axis=-1) @ w_out
    if not ctx_out:
        return None, y_x
    att_c = context_attention(qc, kc, vc, sink)
    conv_c = bc * short_conv(cc * uc, conv_w, conv_b)
    y_c = jnp.concatenate([att_c, conv_c], axis=-1) @ w_out
    return y_c, y_x


def chunk_retention(q, k, v, log_g, s0, include_diag):
    B, L, H, dk = q.shape
    dv = v.shape[-1]
    C = RET_CHUNK
    n = L // C
    qc = q.astype(jnp.float32).reshape(B, n, C, H, dk)
    kc = k.astype(jnp.float32).reshape(B, n, C, H, dk)
    vc = v.astype(jnp.float32).reshape(B, n, C, H, dv)
    i = jnp.arange(C, dtype=jnp.float32)
    rel = i[:, None] - i[None, :]
    keep = (rel >= 0) if include_diag else (rel > 0)
    dmask = jnp.where(keep[None], jnp.exp(log_g[:, None, None] * jnp.where(keep, rel, 0.0)[None]), 0.0)
    inner = jnp.einsum('bnihd,bnjhd->bnhij', qc, kc) * dmask
    y = jnp.einsum('bnhij,bnjhe->bnihe', inner, vc)
    zeta = jnp.exp(log_g[None, :] * (C - 1 - i)[:, None])
    kv = jnp.einsum('bnjhd,jh,bnjhe->nbhde', kc, zeta, vc)
    g_chunk = jnp.exp(log_g * C)[None, :, None, None]

    def step(state, kv_n):
        return g_chunk * state + kv_n, state

    s_final, s_prev = lax.scan(step, s0, kv)
    xi = jnp.exp(log_g[None, :] * (i + 1.0)[:, None])
    y = y + jnp.einsum('bnihd,nbhde->bnihe', qc, s_prev) * xi[None, None, :, :, None]
    return y.reshape(B, L, H, dv), s_final


def retention_out(y, g, gn_w):
    B, L, H, dv = y.shape
    mu = jnp.mean(y, axis=-1, keepdims=True)
    var = jnp.mean(jnp.square(y - mu), axis=-1, keepdims=True)
    yn = (y - mu) * lax.rsqrt(var + EPS) * gn_w.astype(jnp.float32).reshape(H, dv)
    return (jax.nn.silu(g.astype(jnp.float32)) * yn.reshape(B, L, H * dv)).astype(g.dtype)


def na_indices(length, rows):
    kr = min(NA_ROWS, rows)
    kcn = NA_COLS
    t = jnp.arange(length)
    r = t // GRID_W
    c = t % GRID_W
    rs = jnp.clip(r - kr // 2, 0, rows - kr)
    cs = jnp.clip(c - kcn // 2, 0, GRID_W - kcn)
    kr_idx = rs[:, None] + jnp.arange(kr)[None, :]
    kc_idx = cs[:, None] + jnp.arange(kcn)[None, :]
    kidx = (kr_idx[:, :, None] * GRID_W + kc_idx[:, None, :]).reshape(length, kr * kcn)
    dr = kr_idx - r[:, None] + NA_ROWS - 1
    dc = kc_idx - c[:, None] + NA_COLS - 1
    bidx = (dr[:, :, None] * (2 * NA_COLS - 1) + dc[:, None, :]).reshape(length, kr * kcn)
    return kidx, bidx


def neighbourhood_attention_latent(q, k, v, kc, vc, rpb, kidx, bidx):
    B, S, H, d = q.shape
    nblk = S // NA_BLOCK
    nk = kidx.shape[-1]
    scale = d ** -0.5
    rpb_flat = rpb.reshape(H, -1).astype(jnp.float32)
    qb = q.reshape(B, nblk, NA_BLOCK, H, d).transpose(1, 0, 2, 3, 4)
    kib = kidx.reshape(nblk, NA_BLOCK, nk)
    bib = bidx.reshape(nblk, NA_BLOCK, nk)

    def block(args):
        qq, ki, bi = args
        kg = k[:, ki]
        vg = v[:, ki]
        s_loc = jnp.einsum('bqhd,bqkhd->bhqk', qq, kg).astype(jnp.float32) * scale + rpb_flat[:, bi][None]
        s_ctx = jnp.einsum('bqhd,bchd->bhqc', qq, kc).astype(jnp.float32) * scale
        p = jax.nn.softmax(jnp.concatenate([s_loc, s_ctx], axis=-1), axis=-1).astype(v.dtype)
        return (jnp.einsum('bhqk,bqkhd->bqhd', p[..., :nk], vg)
                + jnp.einsum('bhqc,bchd->bqhd', p[..., nk:], vc))

    o = lax.map(block, (qb, kib, bib))
    return o.transpose(1, 0, 2, 3, 4).reshape(B, S, H * d)


def project_odd(h, w_in):
    B, L, _ = h.shape
    rq, rk, rv, rg, nq, nk, nv = split_cols(h @ w_in, [RET_HEADS * RET_DK, RET_HEADS * RET_DK,
                                                       RET_HEADS * RET_DV, RET_HEADS * RET_DV,
                                                       NA_HEADS * HEAD_DIM, NA_HEADS * HEAD_DIM,
                                                       NA_HEADS * HEAD_DIM])
    return (rq.reshape(B, L, RET_HEADS, RET_DK), rk.reshape(B, L, RET_HEADS, RET_DK) * (RET_DK ** -0.5),
            rv.reshape(B, L, RET_HEADS, RET_DV), rg,
            nq.reshape(B, L, NA_HEADS, HEAD_DIM), nk.reshape(B, L, NA_HEADS, HEAD_DIM),
            nv.reshape(B, L, NA_HEADS, HEAD_DIM))


def mixer_ret_na(h_c, h_x, w_in, w_out, decay_f, decay_b, gn_w, rpb, cos, sin, kidx, bidx, ctx_out):
    rqx, rkx, rvx, rgx, nqx, nkx, nvx = project_odd(h_x, w_in)
    rqc, rkc, rvc, rgc, nqc, nkc, nvc = project_odd(h_c, w_in)
    rqx = apply_rope(rqx, cos, sin)
    rkx = apply_rope(rkx, cos, sin)
    lg_f = jax.nn.log_sigmoid(decay_f.astype(jnp.float32))
    lg_b = jax.nn.log_sigmoid(decay_b.astype(jnp.float32))
    rev = lambda a: jnp.flip(a, axis=1)
    B = h_x.shape[0]
    s0 = jnp.zeros((B, RET_HEADS, RET_DK, RET_DV), jnp.float32)
    yc_f, s_f = chunk_retention(rqc, rkc, rvc, lg_f, s0, True)
    yc_b, s_b = chunk_retention(rev(rqc), rev(rkc), rev(rvc), lg_b, s0, False)
    yx_f, _ = chunk_retention(rqx, rkx, rvx, lg_f, s_f, True)
    yx_b, _ = chunk_retention(rev(rqx), rev(rkx), rev(rvx), lg_b, s_b, False)
    ret_x = retention_out(yx_f + rev(yx_b), rgx, gn_w)
    na_x = neighbourhood_attention_latent(nqx, nkx, nvx, nkc, nvc, rpb, kidx, bidx)
    y_x = jnp.concatenate([ret_x, na_x], axis=-1) @ w_out
    if not ctx_out:
        return None, y_x
    ret_c = retention_out(yc_f + rev(yc_b), rgc, gn_w)
    na_c = context_attention(nqc, nkc, nvc)
    y_c = jnp.concatenate([ret_c, na_c], axis=-1) @ w_out
    return y_c, y_x


def setup_inputs(seed: int = 0) -> dict:
    key = jax.random.key(seed)
    ks = jax.random.split(key, 24)
    f32 = jnp.float32
    D = D_MODEL

    def nrm(k, shape, s):
        return s * jax.random.normal(k, shape, f32)

    dec0 = jnp.log(2.0 ** (5.0 + jnp.arange(RET_HEADS, dtype=f32)) - 1.0)
    return {
        'x': nrm(ks[0], (BATCH, SEQ, D), 1.0),
        'c': nrm(ks[1], (BATCH, D), 1.0),
        'ctx': nrm(ks[2], (BATCH, CTX_LEN, D), 1.0),
        'c_ctx': nrm(ks[3], (D,), 1.0),
        'ada_w': nrm(ks[4], (DEPTH, D, N_MOD * D), D ** -0.5),
        'ada_b': nrm(ks[5], (DEPTH, N_MOD * D), 0.01),
        'norm_w': 1.0 + nrm(ks[6], (DEPTH, 3, D), 0.05),
        'ffn_a_wi': nrm(ks[7], (DEPTH, D, 2 * D_FF), D ** -0.5),
        'ffn_a_wo': nrm(ks[8], (DEPTH, D_FF, D), D_FF ** -0.5),
        'ffn_b_wi': nrm(ks[9], (DEPTH, D, 2 * D_FF), D ** -0.5),
        'ffn_b_wo': nrm(ks[10], (DEPTH, D_FF, D), D_FF ** -0.5),
        'ev_w_in': nrm(ks[11], (N_EVEN, D, EVEN_IN), D ** -0.5),
        'ev_w_out': nrm(ks[12], (N_EVEN, EVEN_OUT, D), EVEN_OUT ** -0.5),
        'ev_sink': nrm(ks[13], (N_EVEN, ATTN_HEADS), 1.0),
        'ev_conv_w': nrm(ks[14], (N_EVEN, CONV_W, CONV_CH), CONV_W ** -0.5),
        'ev_conv_b': nrm(ks[15], (N_EVEN, CONV_CH), 0.02),
        'od_w_in': nrm(ks[16], (N_ODD, D, ODD_IN), D ** -0.5),
        'od_w_out': nrm(ks[17], (N_ODD, ODD_OUT, D), ODD_OUT ** -0.5),
        'od_decay_f': dec0 + nrm(ks[18], (N_ODD, RET_HEADS), 0.1),
        'od_decay_b': dec0 + nrm(ks[19], (N_ODD, RET_HEADS), 0.1),
        'od_gn_w': 1.0 + nrm(ks[20], (N_ODD, RET_HEADS * RET_DV), 0.05),
        'od_rpb': nrm(ks[21], (N_ODD, NA_HEADS, 2 * NA_ROWS - 1, 2 * NA_COLS - 1), 0.02),
        'final_norm_w': 1.0 + nrm(ks[22], (D,), 0.05),
    }


def reference(x, c, ctx, c_ctx, ada_w, ada_b, norm_w, ffn_a_wi, ffn_a_wo, ffn_b_wi, ffn_b_wo,
              ev_w_in, ev_w_out, ev_sink, ev_conv_w, ev_conv_b,
              od_w_in, od_w_out, od_decay_f, od_decay_b, od_gn_w, od_rpb, final_norm_w):
    B, S, D = x.shape
    rows = S // GRID_W
    cos, sin = axial_rope(S, HEAD_DIM)
    na_kidx, na_bidx = na_indices(S, rows)
    hx, hc = x, ctx
    for layer in range(DEPTH):
        last = layer == DEPTH - 1
        j = layer // 2
        mx = (jax.nn.silu(c) @ ada_w[layer] + ada_b[layer]).reshape(B, N_MOD, 1, D).transpose(1, 0, 2, 3)
        mc = (jax.nn.silu(c_ctx) @ ada_w[layer] + ada_b[layer]).reshape(N_MOD, 1, 1, D)
        nw = norm_w[layer]
        hx = ffn_half(hx, nw[0], mx[0], mx[1], mx[2], ffn_a_wi[layer], ffn_a_wo[layer])
        hc = ffn_half(hc, nw[0], mc[0], mc[1], mc[2], ffn_a_wi[layer], ffn_a_wo[layer])
        ux = modulate(rmsnorm(hx, nw[1]), mx[3], mx[4])
        uc = modulate(rmsnorm(hc, nw[1]), mc[3], mc[4])
        if layer % 2 == 0:
            yc, yx = mixer_attn_conv(uc, ux, ev_w_in[j], ev_w_out[j], ev_sink[j], ev_conv_w[j], ev_conv_b[j],
                                     cos, sin, not last)
        else:
            yc, yx = mixer_ret_na(uc, ux, od_w_in[j], od_w_out[j], od_decay_f[j], od_decay_b[j], od_gn_w[j],
                                  od_rpb[j], cos, sin, na_kidx, na_bidx, not last)
        hx = hx + mx[5] * yx
        hx = ffn_half(hx, nw[2], mx[6], mx[7], mx[8], ffn_b_wi[layer], ffn_b_wo[layer])
        if not last:
            hc = hc + mc[5] * yc
            hc = ffn_half(hc, nw[2], mc[6], mc[7], mc[8], ffn_b_wi[layer], ffn_b_wo[layer])
    return rmsnorm(hx, final_norm_w)
```

```python
import numpy as np
from contextlib import ExitStack, contextmanager
import concourse.bass as bass
import concourse.mybir as mybir

F32 = mybir.dt.float32
BF16 = mybir.dt.bfloat16
ALU = mybir.AluOpType
AF = mybir.ActivationFunctionType

ENGS = ("pe", "act", "dve", "pool", "sp")


class _St:
    __slots__ = ("w", "r")

    def __init__(self, init):
        self.w = None
        self.r = dict(init)


class Tile:
    def __init__(self, em, h, name, init):
        self.em = em
        self.h = h
        self.name = name
        self.init = init
        self.st = {}
        self.dsem = {}

    def state(self, k):
        s = self.st.get(k)
        if s is None:
            s = self.st[k] = _St(self.init)
        return s

    def v(self, idx=None, keys=(None,)):
        ap = self.h[:] if idx is None else self.h[idx]
        if not isinstance(keys, (tuple, list)):
            keys = (keys,)
        return View(self, ap, tuple(keys))

    def __getitem__(self, idx):
        return View(self, self.h[idx], (None,))


class View:
    __slots__ = ("tile", "ap", "keys")

    def __init__(self, tile, ap, keys):
        self.tile = tile
        self.ap = ap
        self.keys = keys


def _ap(x):
    return x.ap if isinstance(x, View) else x


class Emitter:
    def __init__(self, nc):
        self.nc = nc
        self.stack = ExitStack()
        self.eng = {"pe": nc.tensor, "act": nc.scalar, "dve": nc.vector, "pool": nc.gpsimd, "sp": nc.sync}
        self.sem = {}
        self.count = {e: 0 for e in ENGS}
        self.seen = {e: {} for e in ENGS}
        self.freed = {}
        self.nsem = 0
        self.scopes = []
        for e in ("pe", "act", "dve", "pool"):
            self.sem[e] = self.stack.enter_context(nc.semaphore("s_" + e))
        self.ninstr = 0

    def new_sem(self, name):
        self.nsem += 1
        return self.stack.enter_context(self.nc.semaphore("d%d_%s" % (self.nsem, name)))

    def _mk(self, h, name):
        t = Tile(self, h, name, dict(self.freed))
        if self.scopes:
            self.scopes[-1][1].append(t)
        return t

    def sbuf(self, name, shape, dtype):
        st = self.scopes[-1][0] if self.scopes else self.stack
        self.nsem += 1
        name = "%s_%d" % (name, self.nsem)
        h = st.enter_context(self.nc.sbuf_tensor(name, list(shape), dtype))
        return self._mk(h, name)

    def psum(self, name, shape, dtype):
        st = self.scopes[-1][0] if self.scopes else self.stack
        h = st.enter_context(self.nc.psum_tensor(name, list(shape), dtype))
        return self._mk(h, name)

    def dram(self, name, shape, dtype, kind=None):
        if kind is None:
            h = self.nc.dram_tensor(name, list(shape), dtype)
        else:
            h = self.nc.dram_tensor(name, list(shape), dtype, kind=kind)
        return Tile(self, h, name, {})

    @contextmanager
    def scope(self):
        st = ExitStack()
        tiles = []
        self.scopes.append((st, tiles))
        try:
            yield
        finally:
            self.scopes.pop()
            for t in tiles:
                for s in t.st.values():
                    evs = list(s.r.values())
                    if s.w is not None:
                        evs.append(s.w)
                    for (sem, val) in evs:
                        k = id(sem)
                        if k not in self.freed or self.freed[k][1] < val:
                            self.freed[k] = (sem, val)
            st.close()

    def _deps(self, eng, reads, writes):
        deps = {}

        def add(ev):
            if ev is None:
                return
            sem, val = ev
            k = id(sem)
            if k not in deps or deps[k][1] < val:
                deps[k] = (sem, val)

        for v in reads:
            if not isinstance(v, View):
                continue
            for k in v.keys:
                add(v.tile.state(k).w)
        for v in writes:
            if not isinstance(v, View):
                continue
            for k in v.keys:
                s = v.tile.state(k)
                add(s.w)
                for ev in s.r.values():
                    add(ev)
        waits = []
        seen = self.seen[eng]
        for k, (sem, val) in deps.items():
            if eng == "pe" and sem is self.sem.get("pe"):
                continue
            if seen.get(k, 0) < val:
                seen[k] = val
                waits.append((sem, val))
        return waits

    def _mark(self, ev, reads, writes):
        k0 = id(ev[0])
        for v in reads:
            if not isinstance(v, View):
                continue
            for k in v.keys:
                s = v.tile.state(k)
                if k0 not in s.r or s.r[k0][1] < ev[1]:
                    s.r[k0] = ev
        for v in writes:
            if not isinstance(v, View):
                continue
            for k in v.keys:
                s = v.tile.state(k)
                s.w = ev
                s.r = {}

    def op(self, eng, fn, reads, writes, inc=True):
        waits = self._deps(eng, reads, writes)
        mysem = self.sem[eng]
        ev = (mysem, self.count[eng] + 1)
        if inc:
            self.count[eng] += 1

        e = self.eng[eng]
        for (sem, val) in waits:
            e.wait_ge(sem, val)
        ins = fn(e)
        if inc:
            ins.then_inc(mysem, 1)
        self._mark(ev, reads, writes)
        self.ninstr += 1

    @contextmanager
    def dma_group(self, name):
        g = {"sem": self.new_sem(name), "total": 0, "marks": []}
        self._grp = g
        try:
            yield g
        finally:
            self._grp = None
            ev = (g["sem"], g["total"])
            for (reads, writes) in g["marks"]:
                self._mark(ev, reads, writes)

    def dma(self, eng, out, in_, **kw):
        reads = [in_]
        writes = [out]
        waits = self._deps(eng, reads, writes)
        g = getattr(self, "_grp", None)
        if g is not None:
            kw.pop("semof", None)
            e = self.eng[eng]
            for (s, val) in waits:
                e.wait_ge(s, val)
            e.dma_start(out=_ap(out), in_=_ap(in_), **kw).then_inc(g["sem"], 16)
            g["total"] += 16
            g["marks"].append((reads, writes))
            self.ninstr += 1
            return None
        semof = kw.pop("semof", None)
        tgt = semof if semof is not None else (out if isinstance(out, View) else in_)
        key = (tgt.keys[0])
        t = tgt.tile
        if key not in t.dsem:
            t.dsem[key] = [self.new_sem(t.name), 0]
        rec = t.dsem[key]
        rec[1] += 16
        ev = (rec[0], rec[1])
        o_ap, i_ap = _ap(out), _ap(in_)

        e = self.eng[eng]
        for (s, val) in waits:
            e.wait_ge(s, val)
        e.dma_start(out=o_ap, in_=i_ap, **kw).then_inc(rec[0], 16)
        self._mark(ev, reads, writes)
        self.ninstr += 1
        return ev

    def wait_all(self, eng, views):
        waits = self._deps(eng, [], views)

        e = self.eng[eng]
        for (s, val) in waits:
            e.wait_ge(s, val)

    def mm(self, out, lhsT, rhs, start=True, stop=True, inc=None):
        if inc is None:
            inc = stop
        o, l, r = _ap(out), _ap(lhsT), _ap(rhs)
        self.op("pe", lambda e: e.matmul(o, l, r, start=start, stop=stop), [lhsT, rhs], [out], inc=inc)

    def transpose(self, out, in_, ident, inc=True):
        o, i, d = _ap(out), _ap(in_), _ap(ident)
        self.op("pe", lambda e: e.transpose(o, i, d), [in_, ident], [out], inc=inc)

    def act(self, out, in_, func, bias=None, scale=None, eng="act"):
        o, i = _ap(out), _ap(in_)
        kw = {}
        rd = [in_]
        if bias is not None:
            kw["bias"] = _ap(bias)
            rd.append(bias)
        if scale is not None:
            kw["scale"] = _ap(scale)
            rd.append(scale)
        self.op(eng, lambda e: e.activation(o, i, func, **kw), rd, [out])

    def tt(self, eng, out, in0, in1, op):
        o, a, b = _ap(out), _ap(in0), _ap(in1)
        self.op(eng, lambda e: e.tensor_tensor(o, a, b, op), [in0, in1], [out])

    def ts(self, eng, out, in0, s1, s2=None, op0=ALU.mult, op1=None):
        o, a, x1, x2 = _ap(out), _ap(in0), _ap(s1), _ap(s2)
        rd = [in0, s1, s2]
        if op1 is None:
            self.op(eng, lambda e: e.tensor_scalar(o, a, x1, x2, op0), rd, [out])
        else:
            self.op(eng, lambda e: e.tensor_scalar(o, a, x1, x2, op0, op1), rd, [out])

    def stt(self, eng, out, in0, scalar, in1, op0, op1):
        o, a, s, b = _ap(out), _ap(in0), _ap(scalar), _ap(in1)
        self.op(eng, lambda e: e.scalar_tensor_tensor(o, a, s, b, op0, op1), [in0, scalar, in1], [out])

    def copy(self, eng, out, in_):
        o, i = _ap(out), _ap(in_)
        if eng == "act":
            self.op(eng, lambda e: e.activation(o, i, AF.Identity), [in_], [out])
        else:
            self.op(eng, lambda e: e.tensor_copy(o, i), [in_], [out])

    def memset(self, eng, out, val):
        o = _ap(out)
        self.op(eng, lambda e: e.memset(o, val), [], [out])

    def recip(self, out, in_):
        o, i = _ap(out), _ap(in_)
        self.op("dve", lambda e: e.reciprocal(o, i), [in_], [out])

    def finish(self, final_views=(), close=True):
        if final_views:
            self.wait_all("sp", list(final_views))
        if close:
            self.stack.close()


from concourse.bass_utils import run_bass_kernel_spmd

D = 2048
KC = 16
NCORE = 8
LAT = 1024
CTX = 256
T0 = CTX + LAT
DFF = 5632
FC = 44
EPS = 1e-6
SL = slice(None)


def ckeys(t, k, lo, n):
    g = getattr(t, "grid", None)
    if g is None:
        return tuple((k, ti) for ti, (tl, tn) in enumerate(t.tt) if tl < lo + n and lo < tl + tn)
    return tuple((k, ("g", i)) for i in range(len(g) - 1) if g[i] < lo + n and lo < g[i + 1])


def kt(t, k, ti):
    lo, n = t.tt[ti]
    return t.v((SL, k, slice(lo, lo + n)), keys=ckeys(t, k, lo, n))


def allk(t, k):
    w = t.h[:].shape[-1]
    return t.v((SL, k), keys=ckeys(t, k, 0, w))


class Ring:
    def __init__(self, em, name, shape, dtype, depth):
        self.em = em
        self.t = em.sbuf(name, [128, depth] + list(shape), dtype)
        self.depth = depth
        self.i = 0

    def load(self, src, eng="pool"):
        s = self.i % self.depth
        self.i += 1
        self.em.dma(eng, self.t.v((SL, s), keys=(s,)), src)
        return s

    def view(self, s, idx=()):
        return self.t.v((SL, s) + tuple(idx), keys=(s,))


def stream(ring, srcs, ahead):
    slots = []
    n = len(srcs)
    for i in range(min(ahead, n)):
        slots.append(ring.load(srcs[i]))
    for i in range(n):
        if i + ahead < n:
            slots.append(ring.load(srcs[i + ahead]))
        yield i, slots[i]


class Ctx:
    def __init__(self, nc):
        self.nc = nc
        self.em = em = Emitter(nc)
        self.ps = [em.psum("ps%d" % i, [128, 512], F32) for i in range(8)]
        self.ones = em.sbuf("ones_bf", [128, 128], BF16)
        em.memset("dve", self.ones.v(), 1.0)
        self.epsc = em.sbuf("epsc", [128, 1], F32)
        em.memset("dve", self.epsc.v(), EPS)


def rmsnorm_mod(cx, H, xn, ranges, scratch_ps, out_fn=None):
    em = cx.em
    with em.scope():
        rstd = em.sbuf("rstd", [128, 512], F32)
        ntmp = [em.sbuf("ntmp%d" % i, [128, 512], F32) for i in range(2)]
        for (hti, xti, Af, Bf) in ranges:
            n = H.tt[hti][1]
            ps = cx.ps[scratch_ps]
            for k in range(KC):
                if k % 2 == 0:
                    em.act(kt(xn, k, xti), kt(H, k, hti), AF.Square)
                else:
                    em.tt("pool", kt(xn, k, xti), kt(H, k, hti), kt(H, k, hti), ALU.mult)
            for k in range(KC):
                em.mm(ps.v((SL, slice(0, n))), cx.ones.v(), kt(xn, k, xti), start=(k == 0), stop=(k == KC - 1))
            em.act(rstd.v((SL, slice(0, n))), ps.v((SL, slice(0, n))), AF.Sqrt, bias=cx.epsc.v(), scale=1.0 / D)
            em.recip(rstd.v((SL, slice(0, n))), rstd.v((SL, slice(0, n))))
            for k in range(KC):
                tmp = ntmp[k % 2]
                em.tt("dve", tmp.v((SL, slice(0, n))), kt(H, k, hti), rstd.v((SL, slice(0, n))), ALU.mult)
                if out_fn is not None:
                    out_fn(k, xti, tmp.v((SL, slice(0, n))), n)
                elif isinstance(Af, list):
                    xlo = xn.tt[xti][0]
                    for (off, ns, Afs, Bfs) in Af:
                        em.act(xcols(xn, k, xlo + off, ns), tmp.v((SL, slice(off, off + ns))), AF.Identity, bias=Bfs(k), scale=Afs(k))
                else:
                    em.act(kt(xn, k, xti), tmp.v((SL, slice(0, n))), AF.Identity, bias=Bf(k), scale=Af(k))


def ffn(cx, H, xn, tiles, wi_d, wo_d, gh, G=4):
    em = cx.em
    with em.scope():
        wi_r = Ring(em, "wi_r", [KC, 2, 128], BF16, 3)
        wo_r = Ring(em, "wo_r", [D], BF16, 2 * G)
        ntt = len(tiles)
        actb = em.sbuf("actb", [128, 2 * G, max(lo_ + n_ for lo_, n_ in xn.tt)], BF16)
        sg = [em.sbuf("sg%d" % i, [128, 512], F32) for i in range(2)]
        wo_slots = {}
        wo_next = 0
        cnt = 0
        for f, ws in stream(wi_r, [wi_d[f] for f in range(FC)], 2):
            while wo_next < min(FC, f + G + 1):
                wo_slots[wo_next] = wo_r.load(wo_d[wo_next])
                wo_next += 1
            aslot = f % (2 * G)
            for (hti, xti, vec) in tiles:
                lo, n = xn.tt[xti]
                b = (cnt % 2) * 2
                cnt += 1
                gp, up = cx.ps[b], cx.ps[b + 1]
                for k in range(KC):
                    em.mm(gp.v((SL, slice(0, n))), wi_r.view(ws, (k, 0)), kt(xn, k, xti), start=(k == 0), stop=(k == KC - 1))
                for k in range(KC):
                    em.mm(up.v((SL, slice(0, n))), wi_r.view(ws, (k, 1)), kt(xn, k, xti), start=(k == 0), stop=(k == KC - 1))
                s = sg[cnt % 2]
                em.act(s.v((SL, slice(0, n))), gp.v((SL, slice(0, n))), AF.Silu)
                em.tt("dve", actb.v((SL, aslot, slice(lo, lo + n)), keys=((aslot, xti),)), s.v((SL, slice(0, n))),
                      up.v((SL, slice(0, n))), ALU.mult)
            if f % G == G - 1:
                f0 = f - G + 1
                oc = 0
                for d in range(KC):
                    for (hti, xti, vec) in tiles:
                        lo, n = xn.tt[xti]
                        op = cx.ps[4 + (oc % 2)]
                        oc += 1
                        for j in range(G):
                            ff = f0 + j
                            em.mm(op.v((SL, slice(0, n))), wo_r.view(wo_slots[ff], (slice(d * 128, (d + 1) * 128),)),
                                  actb.v((SL, ff % (2 * G), slice(lo, lo + n)), keys=((ff % (2 * G), xti),)),
                                  start=(j == 0), stop=(j == G - 1))
                        if isinstance(vec, list):
                            hlo = H.tt[hti][0]
                            for (off, ns, vv) in vec:
                                hv = xcols(H, d, hlo + off, ns)
                                em.stt("dve", hv, op.v((SL, slice(off, off + ns))), gh(vv, d), hv, ALU.mult, ALU.add)
                        else:
                            em.stt("dve", kt(H, d, hti), op.v((SL, slice(0, n))), gh(vec, d), kt(H, d, hti), ALU.mult, ALU.add)


def norm_scratch(cx):
    pass


def mk_ranges(tiles, tvec, Aof, Bof):
    out = []
    for ti, tv in zip(tiles, tvec):
        if isinstance(tv, list):
            out.append((ti, ti, [(off, ns, Aof(v), Bof(v)) for (off, ns, v) in tv], None))
        else:
            out.append((ti, ti, Aof(tv), Bof(tv)))
    return out


def prep_mod(cx, mod_d, nw_d, nsub):
    em = cx.em
    m = em.sbuf("modt", [128, 2, nsub * 3, KC], F32)
    nw = em.sbuf("nwt", [128, nsub, KC], F32)
    A = em.sbuf("modA", [128, 2, nsub, KC], F32)
    em.dma("sp", m.v(), mod_d)
    em.dma("sp", nw.v(), nw_d)
    for v in range(2):
        for s in range(nsub):
            em.ts("dve", A.v((SL, v, s)), m.v((SL, v, 3 * s + 1)), 1.0, None, ALU.add)
            em.tt("dve", A.v((SL, v, s)), A.v((SL, v, s)), nw.v((SL, s)), ALU.mult)
    return m, A


TT1280 = [(0, 256), (256, 512), (768, 512)]
CO = CTX // NCORE
TA = CO + LAT
TW = TA // 3
TTA = [(0, TW), (TW, TW), (2 * TW, TW)]
GRID_A = [0, CO, TW, 2 * TW, TA]
CB = CTX - CO
TTF = [(CB, TW), (CB + TW, TW), (CB + 2 * TW, TW)]
GRID_B = sorted(set([0, CB, 256, 768, T0] + [CB + i * TW for i in range(4)]))
SEG0 = [(0, CO, 1), (CO, TW - CO, 0)]


def build_A():
    nc = bass.Bass("TRN2", target_bir_lowering=False)
    cx = Ctx(nc)
    em = cx.em
    xT = nc.dram_tensor("xT", [D, TA], F32, kind="ExternalInput")
    mod = nc.dram_tensor("mod", [128, 2, 3, KC], F32, kind="ExternalInput")
    nw = nc.dram_tensor("nw", [128, 1, KC], F32, kind="ExternalInput")
    wi = nc.dram_tensor("wi", [FC, 128, KC, 2, 128], F32, kind="ExternalInput")
    wo = nc.dram_tensor("wo", [FC, 128, D], F32, kind="ExternalInput")
    hout = em.dram("hout", [D, TA], F32, kind="ExternalOutput")
    norm_scratch(cx)
    H = em.sbuf("H", [128, KC, TA], F32)
    H.tt = TTA
    H.grid = GRID_A
    xn = em.sbuf("xn", [128, KC, TA], BF16)
    xn.tt = TTA
    xn.grid = GRID_A
    xT_v = xT.ap().rearrange("(k p) t -> p k t", p=128)
    with em.dma_group("Hld"):
        for k in range(KC):
            em.dma("sp", allk(H, k), xT_v[:, k, :])
    m, A = prep_mod(cx, mod[:, :, :, :], nw[:, :, :], 1)
    vec_of = [SEG0, 0, 0]
    rmsnorm_mod(cx, H, xn, mk_ranges(range(3), vec_of, lambda v: (lambda k: A.v((SL, v, 0, slice(k, k + 1)))),
                                     lambda v: (lambda k: m.v((SL, v, 0, slice(k, k + 1))))), 6)
    gh = em.sbuf("gh", [128, 2, KC], F32)
    em.ts("dve", gh.v(), m.v((SL, SL, 2)), 0.5, None, ALU.mult)
    ffn(cx, H, xn, [(ti, ti, vec_of[ti]) for ti in range(3)], wi, wo,
        lambda vec, d: gh.v((SL, vec, slice(d, d + 1))))
    ho_v = hout.h.ap().rearrange("(k p) t -> p k t", p=128)
    with em.dma_group("hst"):
        for k in range(KC):
            em.dma("sp", View(hout, ho_v[:, k, :], ((k,),)), allk(H, k))
    em.finish([View(hout, hout.h[:, :], tuple((k,) for k in range(KC)))])
    return nc


def lay_wi(w):
    return np.ascontiguousarray(w.reshape(KC, 128, 2, FC, 128).transpose(3, 1, 0, 2, 4))


def lay_wo(w):
    return np.ascontiguousarray(w.reshape(FC, 128, D))


def lay_vec(v):
    v = np.asarray(v)
    lead = v.shape[:-1]
    r = v.reshape(lead + (KC, 128))
    return np.ascontiguousarray(np.moveaxis(r, -1, 0))


def xcols(xn, k, lo, n):
    return xn.v((SL, k, slice(lo, lo + n)), keys=ckeys(xn, k, lo, n))


def proj_fm(cx, xn, xtis, wsrcs, ring, evac, banks=(0, 1, 2, 3)):
    em = cx.em
    cnt = 0
    for ci, ws in stream(ring, wsrcs, ring.depth - 1):
        for xti in xtis:
            lo, n = xn.tt[xti]
            ps = cx.ps[banks[cnt % len(banks)]]
            cnt += 1
            for k in range(KC):
                em.mm(ps.v((SL, slice(0, n))), ring.view(ws, (k,)), kt(xn, k, xti), start=(k == 0), stop=(k == KC - 1))
            evac(ci, xti, ps, n)


def proj_tm(cx, xn, blocks, wsrcs, wt, evac, banks=(0, 1, 2, 3)):
    em = cx.em
    nch = len(wsrcs)
    for c, src in enumerate(wsrcs):
        em.dma("pool", wt.v((SL, c), keys=(c,)), src)
    wkeys = tuple(range(nch))
    for bi, lo in enumerate(blocks):
        ps = cx.ps[banks[bi % len(banks)]]
        for k in range(KC):
            em.mm(ps.v((SL, slice(0, nch * 128))), xcols(xn, k, lo, 128), wt.v((SL, SL, k, SL), keys=wkeys),
                  start=(k == 0), stop=(k == KC - 1))
        evac(bi, ps)


ROPE_ADD_ENG = "dve"


def rope_evac(cx, ps, n, out, cosv, sinv):
    em = cx.em
    i = cx.rcnt % 2
    cx.rcnt += 1
    xb = cx.rxb[i]
    em.copy("act", xb.v((SL, slice(0, n))), ps.v((SL, slice(0, n))))
    sw = cx.ps[6 + i]
    em.mm(sw.v((SL, slice(0, n))), cx.Pm.v(), xb.v((SL, slice(0, n))))
    t1, t2 = cx.rt1[i], cx.rt2[i]
    em.tt("dve", t1.v((SL, slice(0, n))), xb.v((SL, slice(0, n))), cosv, ALU.mult)
    em.tt("dve", t2.v((SL, slice(0, n))), sw.v((SL, slice(0, n))), sinv, ALU.mult)
    em.tt(ROPE_ADD_ENG, out, t1.v((SL, slice(0, n))), t2.v((SL, slice(0, n))), ALU.add)


def rope_setup(cx, cos_d, sin_d, pm_d, ncols):
    em = cx.em
    cx.rcnt = 0
    cx.COS = em.sbuf("COS", [128, ncols], F32)
    cx.SIN = em.sbuf("SIN", [128, ncols], F32)
    cx.Pm = em.sbuf("Pm", [128, 128], BF16)
    em.dma("sp", cx.COS.v(), cos_d)
    em.dma("sp", cx.SIN.v(), sin_d)
    em.dma("pool", cx.Pm.v(), pm_d)
    cx.rxb = [em.sbuf("rxb%d" % i, [128, 512], BF16) for i in range(2)]
    cx.rt1 = [em.sbuf("rt1%d" % i, [128, 512], F32) for i in range(2)]
    cx.rt2 = [em.sbuf("rt2%d" % i, [128, 512], F32) for i in range(2)]


def attn_tile(cx, q4, qh, slots, scale, esink4, out4, shared):
    em = cx.em
    ns = len(slots)
    pts = []
    for si, sl in enumerate(slots):
        sp = cx.ps[si % 2]
        if shared:
            em.mm(sp.v(), sl["k"], q4)
        else:
            for hh in range(4):
                em.mm(sp.v((SL, slice(hh * 128, (hh + 1) * 128))), sl["k"](hh), qh(hh), inc=(hh == 3))
        pt = cx.ptr.t.v((SL, cx.ptr.i % cx.ptr.depth), keys=(cx.ptr.i % cx.ptr.depth,))
        cx.ptr.i += 1
        if sl.get("bias") is not None:
            tmp = cx.atmp[si % 2]
            em.stt("dve", tmp.v(), sp.v(), scale, sl["bias"], ALU.mult, ALU.add)
            em.act(pt, tmp.v(), AF.Exp)
        elif sl.get("mask") is not None:
            tmp = cx.atmp[si % 2]
            em.act(tmp.v(), sp.v(), AF.Exp, scale=scale)
            em.tt("dve", pt, tmp.v(), sl["mask"], ALU.mult)
        else:
            em.act(pt, sp.v(), AF.Exp, scale=scale)
        pts.append(pt)
    a = cx.acnt % 2
    cx.acnt += 1
    den, o = cx.ps[2 + a], cx.ps[4 + a]
    for si in range(ns):
        em.mm(den.v(), cx.ones.v(), pts[si], start=(si == 0), stop=(si == ns - 1))
    if shared:
        for si, sl in enumerate(slots):
            em.mm(o.v(), sl["v"], pts[si], start=(si == 0), stop=(si == ns - 1))
    else:
        for hh in range(4):
            for si, sl in enumerate(slots):
                cs = slice(hh * 128, (hh + 1) * 128)
                em.mm(o.v((SL, cs)), sl["v"](hh), View(pts[si].tile, pts[si].ap[:, cs], pts[si].keys),
                      start=(si == 0), stop=(si == ns - 1), inc=(hh == 3 and si == ns - 1))
    rd = cx.rden[a]

    def v3(t):
        return View(t, t.h[:].rearrange("p (h q) -> p h q", h=4), (None,))

    if esink4 is not None:
        em.tt("dve", v3(rd), v3(den), esink4, ALU.add)
        em.recip(rd.v(), rd.v())
    else:
        em.recip(rd.v(), den.v())
    em.tt("dve", out4, v3(o), v3(rd), ALU.mult)


def attn_setup(cx):
    em = cx.em
    cx.acnt = 0
    cx.ptr = Ring(em, "ptr", [512], BF16, 8)
    cx.atmp = [em.sbuf("atmp%d" % i, [128, 512], F32) for i in range(2)]
    cx.rden = [em.sbuf("rden%d" % i, [128, 512], F32) for i in range(2)]


TTB = [(0, 256), (256, 512), (768, 512), (1280, 256)]
ATT_SCALE = 128.0 ** -0.5


def build_B(upto=4):
    nc = bass.Bass("TRN2", target_bir_lowering=False)
    cx = Ctx(nc)
    em = cx.em
    din = lambda name, shape: nc.dram_tensor(name, list(shape), F32, kind="ExternalInput")
    hin = din("hin", [D, T0])
    hhalo = din("hhalo", [D, 256])
    mod0 = din("mod0", [128, 2, 6, KC])
    nw0 = din("nw0", [128, 2, KC])
    win = din("win", [36, 128, KC, 128])
    cosd, sind = din("cos", [128, LAT + 256]), din("sin", [128, LAT + 256])
    pmd = din("pm", [128, 128])
    masks = din("masks", [128, 4, 128])
    sinkd = din("sink", [8])
    convd = din("convp", [128, 8, 4])
    edged = din("edge", [128, 2])
    hout = em.dram("hout", [D, T0], F32, kind="ExternalOutput")
    lfb = em.dram("lfb", [2, 4, 128, 256], F32, kind="ExternalOutput") if upto >= 4 else None
    norm_scratch(cx)
    XB = em.sbuf("XB", [128, KC, T0], BF16)
    XB.tt = TT1280
    XB.grid = GRID_B
    m0, A0 = prep_mod(cx, mod0[:, :, :, :], nw0[:, :, :], 2)
    vec_of = [1, 0, 0, 0]
    hin_v = hin.ap().rearrange("(k p) t -> p k t", p=128)
    hh_v = hhalo.ap().rearrange("(k p) t -> p k t", p=128)

    if upto < 10:
        with em.scope():
            xm = em.sbuf("xm", [128, KC, 1536], BF16)
            xm.tt = TTB
            with em.scope():
                H1 = em.sbuf("H1", [128, KC, 1536], F32)
                H1.tt = TTB
                with em.dma_group("H1ld"):
                    for k in range(KC):
                        em.dma("sp", H1.v((SL, k, slice(0, T0)), keys=tuple((k, ti) for ti in range(3))), hin_v[:, k, :])
                        em.dma("sp", kt(H1, k, 3), hh_v[:, k, :])
                rmsnorm_mod(cx, H1, xm, [(ti, ti, (lambda k, v=vec_of[ti]: A0.v((SL, v, 0, slice(k, k + 1)))),
                                          (lambda k, v=vec_of[ti]: m0.v((SL, v, 0, slice(k, k + 1))))) for ti in range(4)], 6)
            with em.scope():
                rope_setup(cx, cosd[:, :], sind[:, :], pmd[:, :], LAT + 256)
                attn_setup(cx)
                ring = Ring(em, "wring", [KC, 128], BF16, 4)
                QT = em.sbuf("QT", [128, 8, T0], BF16)
                KT = em.sbuf("KT", [128, 2, 1536], BF16)
                V = em.sbuf("V", [128, 12, 256], BF16)
                mk = em.sbuf("mk", [128, 4, 128], F32)
                em.dma("sp", mk.v(), masks[:, :, :])
                es = em.sbuf("es", [128, 8], F32)
                em.dma("sp", es.v(), sinkd.ap().partition_broadcast(128))
                em.act(es.v(), es.v(), AF.Exp)
                cvp = em.sbuf("cvp", [128, 8, 4], F32)
                em.dma("sp", cvp.v(), convd[:, :, :])
                edg = em.sbuf("edg", [128, 2], F32)
                em.dma("sp", edg.v(), edged[:, :])

                def cs_views(xti):
                    lo, n = TTB[xti]
                    return cx.COS.v((SL, slice(lo - 256, lo - 256 + n))), cx.SIN.v((SL, slice(lo - 256, lo - 256 + n)))

                if upto <= -4:
                    return _finish_B(cx, nc, XB, None, hout, lfb, stage=0)
                def evac_k(ci, xti, ps, n):
                    lo, _ = TTB[xti]
                    out = KT.v((SL, ci, slice(lo, lo + n)), keys=((ci, xti),))
                    if xti == 0:
                        em.copy("act", out, ps.v((SL, slice(0, n))))
                    else:
                        c, s_ = cs_views(xti)
                        rope_evac(cx, ps, n, out, c, s_)
                proj_fm(cx, xm, [0, 1, 2, 3], [win[8 + c] for c in range(2)], ring, evac_k)

                if upto <= -3:
                    return _finish_B(cx, nc, XB, None, hout, lfb, stage=0)
                with em.scope():
                    wv = em.sbuf("wv", [128, 2, KC, 128], BF16)

                    def evac_v(bi, ps):
                        em.copy("act", V.v((SL, bi), keys=(bi,)), ps.v((SL, slice(0, 256))))
                    proj_tm(cx, xm, [128 * b for b in range(12)], [win[10 + c] for c in range(2)], wv, evac_v)

                def evac_q(ci, xti, ps, n):
                    lo, _ = TTB[xti]
                    out = QT.v((SL, ci, slice(lo, lo + n)), keys=((ci, xti),))
                    if xti == 0:
                        em.copy("act", out, ps.v((SL, slice(0, n))))
                    else:
                        c, s_ = cs_views(xti)
                        rope_evac(cx, ps, n, out, c, s_)
                proj_fm(cx, xm, [0, 1, 2], [win[c] for c in range(8)], ring, evac_q)

                if upto <= -2:
                    return _finish_B(cx, nc, XB, None, hout, lfb, stage=0)
                with em.scope():
                    csb = em.sbuf("csb", [128, T0], F32)
                    z = em.sbuf("z", [128, T0 + 4], F32)
                    acc = em.sbuf("cacc", [128, T0], F32)
                    c2 = em.sbuf("c2", [128, 2], F32)
                    em.memset("dve", z.v(None, keys=(0, 1, 2, "h")), 0.0)
                    zoff = [1, 3, 3]

                    for j in range(8):
                        st = {}

                        def evac_c(ci, xti, ps, n, st=st):
                            lo, _ = TTB[xti]
                            if ci == 0:
                                em.copy("act", csb.v((SL, slice(lo, lo + n)), keys=(xti,)), ps.v((SL, slice(0, n))))
                            elif ci == 1:
                                em.tt("dve", z.v((SL, slice(lo + zoff[xti], lo + zoff[xti] + n)), keys=(xti,)),
                                      csb.v((SL, slice(lo, lo + n)), keys=(xti,)), ps.v((SL, slice(0, n))), ALU.mult)
                                if xti == 0:
                                    em.tt("dve", z.v((SL, slice(CB, CB + 1)), keys=(0,)), z.v((SL, slice(CB, CB + 1)), keys=(0,)),
                                          edg.v((SL, slice(0, 1))), ALU.mult)
                                    em.tt("dve", z.v((SL, slice(257, 258)), keys=(0,)), z.v((SL, slice(1, 2)), keys=(0,)),
                                          edg.v((SL, slice(1, 2))), ALU.mult)
                            else:
                                if xti == 0:
                                    for (zl, al, n2, zkeys, akey) in ((1, 0, 256, (0,), 0), (259, 256, 512, (1, 2, "h"), 1), (771, 768, 512, (1, 2, "h"), 2)):
                                        w = lambda q: cvp.v((SL, j, slice(q, q + 1)))
                                        a_ = acc.v((SL, slice(al, al + n2)), keys=(akey,))
                                        em.ts("dve", a_, z.v((SL, slice(zl - 1, zl - 1 + n2)), keys=zkeys), w(0), None, ALU.mult)
                                        em.stt("dve", a_, z.v((SL, slice(zl, zl + n2)), keys=zkeys), w(1), a_, ALU.mult, ALU.add)
                                        em.stt("dve", a_, z.v((SL, slice(zl + 1, zl + 1 + n2)), keys=zkeys), w(2), a_, ALU.mult, ALU.add)
                                em.stt("dve", kt(XB, 8 + j, xti), acc.v((SL, slice(lo, lo + n)), keys=(xti,)),
                                       cvp.v((SL, j, slice(3, 4))), ps.v((SL, slice(0, n))), ALU.add, ALU.mult)

                        slots = []
                        srcs = [win[20 + j], win[28 + j], win[12 + j]]
                        cnt = 0
                        for ci, ws in stream(ring, srcs, ring.depth - 1):
                            for xti in [0, 1, 2]:
                                lo, n = TTB[xti]
                                ps = cx.ps[cnt % 4]
                                cnt += 1
                                for k in range(KC):
                                    em.mm(ps.v((SL, slice(0, n))), ring.view(ws, (k,)), kt(xm, k, xti), start=(k == 0), stop=(k == KC - 1))
                                evac_c(ci, xti, ps, n)
                            if ci < 2:
                                ps = cx.ps[cnt % 4]
                                cnt += 1
                                for k in range(KC):
                                    em.mm(ps.v((SL, slice(0, 2))), ring.view(ws, (k,)), xm.v((SL, k, slice(1407, 1409)), keys=((k, 3),)),
                                          start=(k == 0), stop=(k == KC - 1))
                                if ci == 0:
                                    em.copy("act", c2.v(), ps.v((SL, slice(0, 2))))
                                else:
                                    em.tt("dve", c2.v(), c2.v(), ps.v((SL, slice(0, 2))), ALU.mult)
                                    em.tt("dve", c2.v(), c2.v(), edg.v(), ALU.mult)
                                    em.copy("dve", z.v((SL, slice(258, 259)), keys=("h",)), c2.v((SL, slice(0, 1))))
                                    em.copy("dve", z.v((SL, slice(1283, 1284)), keys=("h",)), c2.v((SL, slice(1, 2))))

                if upto <= -1:
                    return _finish_B(cx, nc, XB, None, hout, lfb, stage=0)
                def es4(g):
                    return View(es, es.h[:, 4 * g:4 * g + 4].unsqueeze(2).broadcast_to([128, 4, 128]), (None,))

                def mk4(i):
                    return View(mk, mk.h[:, i, :].unsqueeze(1).broadcast_to([128, 4, 128]), (None,))

                def kslot(g, blk):
                    if blk == "c0" or blk == "c1":
                        b = 0 if blk == "c0" else 1
                        return KT.v((SL, g, slice(128 * b, 128 * b + 128)), keys=((g, 0),)), V.v((SL, b, slice(128 * g, 128 * g + 128)), keys=(b,))
                    if blk == -1:
                        col, vb, key = 1280, 10, (g, 3)
                    elif blk == 8:
                        col, vb, key = 1408, 11, (g, 3)
                    else:
                        col, vb = 256 + 128 * blk, 2 + blk
                        key = (g, 1 if blk < 4 else 2)
                    return KT.v((SL, g, slice(col, col + 128)), keys=(key,)), V.v((SL, vb, slice(128 * g, 128 * g + 128)), keys=(vb,))

                for g in range(2):
                    for qb in range(10):
                        if qb == 0:
                            continue
                        col = 128 * qb
                        xti = 0 if qb < 2 else (1 if qb < 6 else 2)
                        q4 = QT.v((SL, slice(4 * g, 4 * g + 4), slice(col, col + 128)), keys=tuple((4 * g + hh, xti) for hh in range(4)))
                        out4 = XB.v((SL, slice(4 * g, 4 * g + 4), slice(col, col + 128)),
                                    keys=tuple(kk for hh in range(4) for kk in ckeys(XB, 4 * g + hh, col, 128)))
                        slots = []
                        if qb >= 2:
                            n_ = qb - 2
                            kL, vL = kslot(g, n_ - 1)
                            kM, vM = kslot(g, n_)
                            kR, vR = kslot(g, n_ + 1)
                            slots.append(dict(k=kL, v=vL, mask=mk4(0 if n_ == 0 else 1)))
                            slots.append(dict(k=kM, v=vM))
                            slots.append(dict(k=kR, v=vR, mask=mk4(3 if n_ == 7 else 2)))
                        for cb in ("c0", "c1"):
                            kc_, vc_ = kslot(g, cb)
                            slots.append(dict(k=kc_, v=vc_))
                        attn_tile(cx, q4, None, slots, ATT_SCALE, es4(g), out4, True)
    if upto < 1:
        return _finish_B(cx, nc, XB, None, hout, lfb, stage=0)

    wout = din("wout", [16, 128, KC, 128])
    H = em.sbuf("H", [128, KC, T0], F32)
    H.tt = TT1280
    H.grid = GRID_B
    with em.dma_group("Hld"):
        for k in range(KC):
            em.dma("sp", allk(H, k), hin_v[:, k, :])
    vec3 = [1, 0, 0]
    with em.scope():
        ring = Ring(em, "wring2", [KC, 128], BF16, 4)

        def evac_o(ci, xti, ps, n):
            em.stt("dve", kt(H, ci, xti), ps.v((SL, slice(0, n))), m0.v((SL, vec3[xti], 2, slice(ci, ci + 1))), kt(H, ci, xti), ALU.mult, ALU.add)
        if upto < 10:
            proj_fm(cx, XB, [0, 1, 2], [wout[d] for d in range(16)], ring, evac_o)
    if upto < 2:
        return _finish_B(cx, nc, XB, H, hout, lfb, stage=1)

    H.tt = TTF
    XB.tt = TTF
    wib, wob = din("wib", [FC, 128, KC, 2, 128]), din("wob", [FC, 128, D])
    gh = em.sbuf("gh", [128, 2, KC], F32)
    em.ts("dve", gh.v(), m0.v((SL, SL, 5)), 0.5, None, ALU.mult)
    vecF = [SEG0, 0, 0]
    rmsnorm_mod(cx, H, XB, mk_ranges(range(3), vecF, lambda v: (lambda k: A0.v((SL, v, 1, slice(k, k + 1)))),
                                     lambda v: (lambda k: m0.v((SL, v, 3, slice(k, k + 1))))), 6)
    if upto < 10:
        ffn(cx, H, XB, [(ti, ti, vecF[ti]) for ti in range(3)], wib, wob, lambda vec, d: gh.v((SL, vec, slice(d, d + 1))))
    if upto < 3:
        return _finish_B(cx, nc, XB, H, hout, lfb, stage=2)

    wia, woa = din("wia", [FC, 128, KC, 2, 128]), din("woa", [FC, 128, D])
    mod1 = din("mod1", [128, 2, 6, KC])
    nw1 = din("nw1", [128, 2, KC])
    m1, A1 = prep_mod2(cx, mod1[:, :, :, :], nw1[:, :, :], 2)
    gh1 = em.sbuf("gh1", [128, 2, KC], F32)
    em.ts("dve", gh1.v(), m1.v((SL, SL, 2)), 0.5, None, ALU.mult)
    rmsnorm_mod(cx, H, XB, mk_ranges(range(3), vecF, lambda v: (lambda k: A1.v((SL, v, 0, slice(k, k + 1)))),
                                     lambda v: (lambda k: m1.v((SL, v, 0, slice(k, k + 1))))), 6)
    if upto < 10:
        ffn(cx, H, XB, [(ti, ti, vecF[ti]) for ti in range(3)], wia, woa, lambda vec, d: gh1.v((SL, vec, slice(d, d + 1))))
    ho_v = hout.h.ap().rearrange("(k p) t -> p k t", p=128)
    with em.dma_group("hst"):
        for k in range(KC):
            em.dma("sp", View(hout, ho_v[:, k, :], ((k,),)), allk(H, k))
    if upto < 4:
        return _finish_B(cx, nc, XB, None, hout, lfb, stage=3)

    win1 = din("win1", [12, 128, KC, 128])
    decd = din("dec", [2, 4])
    expd = din("expo", [128, 2, 8])
    H.tt = TT1280
    XB.tt = TT1280
    rmsnorm_mod(cx, H, XB, [(ti, ti, (lambda k: A1.v((SL, 0, 1, slice(k, k + 1)))),
                             (lambda k: m1.v((SL, 0, 3, slice(k, k + 1))))) for ti in (1, 2)], 6)
    with em.scope():
        rope_setup(cx, cosd[:, 0:LAT], sind[:, 0:LAT], pmd[:, :], LAT)
        ring = Ring(em, "wring3", [KC, 128], BF16, 4)
        RK = em.sbuf("RK", [128, 4, LAT], BF16)
        RV = em.sbuf("RV", [128, 8, 1024], BF16)
        ret_local_sums(cx, XB, win1, ring, RK, RV, decd, expd, lfb)
    return _finish_B(cx, nc, XB, None, hout, lfb, stage=4)


def prep_mod2(cx, mod_d, nw_d, nsub):
    em = cx.em
    m = em.sbuf("modt2", [128, 2, nsub * 3, KC], F32)
    nw = em.sbuf("nwt2", [128, nsub, KC], F32)
    A = em.sbuf("modA2", [128, 2, nsub, KC], F32)
    em.dma("sp", m.v(), mod_d)
    em.dma("sp", nw.v(), nw_d)
    for v in range(2):
        for s in range(nsub):
            em.ts("dve", A.v((SL, v, s)), m.v((SL, v, 3 * s + 1)), 1.0, None, ALU.add)
            em.tt("dve", A.v((SL, v, s)), A.v((SL, v, s)), nw.v((SL, s)), ALU.mult)
    return m, A


def _finish_B(cx, nc, XB, H, hout, lfb, stage):
    em = cx.em
    finals = []
    if stage < 3:
        ho_v = hout.h.ap().rearrange("(k p) t -> p k t", p=128)
        if H is None:
            with em.scope():
                tmp = em.sbuf("dbg", [128, T0], F32)
                for k in range(KC):
                    em.copy("dve", tmp.v(), allk(XB, k))
                    em.dma("sp", View(hout, ho_v[:, k, :], ((k,),)), tmp.v(), semof=tmp.v())
        else:
            with em.dma_group("hst"):
                for k in range(KC):
                    em.dma("sp", View(hout, ho_v[:, k, :], ((k,),)), allk(H, k))
    finals.append(View(hout, hout.h[:, :], tuple((k,) for k in range(KC))))
    if stage >= 4 and lfb is not None:
        finals.append(View(lfb, lfb.h[:, :, :, :], tuple((d, h) for d in range(2) for h in range(4))))
    em.finish(finals, close=(len(em.scopes) == 0))
    return nc


RET_SCALE = 128.0 ** -0.5


def psbf(ps, n=1024):
    return View(ps, ps.h.bitcast(BF16)[:, 0:n], (None,))


def make_ident(cx):
    em = cx.em
    idf = em.sbuf("identf", [128, 128], F32)
    cx.ident = em.sbuf("ident", [128, 128], BF16)
    em.memset("dve", idf.v(), 1.0)
    em.op("pool", lambda e: e.affine_select(idf.h[:], idf.h[:], [[-1, 128]], ALU.is_equal, 0.0, base=0, channel_multiplier=1),
          [idf.v()], [idf.v()])
    em.copy("dve", cx.ident.v(), idf.v())


def load_lg(cx, decd):
    em = cx.em
    lg = em.sbuf("lg", [128, 8], F32)
    em.dma("sp", lg.v(), decd.ap().rearrange("a b -> (a b)").partition_broadcast(128))
    em.act(lg.v(), lg.v(), AF.Sigmoid)
    em.act(lg.v(), lg.v(), AF.Ln)
    lns = em.sbuf("lns", [128, 1], F32)
    em.memset("dve", lns.v(), float(np.log(RET_SCALE)))
    return lg, lns


def ret_local_sums(cx, XB, win1, ring, RK, RV, decd, expd, lfb):
    em = cx.em
    make_ident(cx)
    lg, lns = load_lg(cx, decd)
    ex = em.sbuf("ex", [128, 2, 8], F32)
    em.dma("sp", ex.v(), expd[:, :, :])
    wts = em.sbuf("wts", [128, 2, 4, 8], F32)
    for d_ in range(2):
        for h in range(4):
            em.act(wts.v((SL, d_, h)), ex.v((SL, d_)), AF.Exp, bias=lns.v(), scale=lg.v((SL, slice(4 * d_ + h, 4 * d_ + h + 1))))

    def evac_k(ci, xti, ps, n):
        lo, _ = XB.tt[xti]
        rope_evac(cx, ps, n, RK.v((SL, ci, slice(lo - 256, lo - 256 + n)), keys=((ci, xti),)),
                  cx.COS.v((SL, slice(lo - 256, lo - 256 + n))), cx.SIN.v((SL, slice(lo - 256, lo - 256 + n))))
    proj_fm(cx, XB, [1, 2], [win1[c] for c in range(4)], ring, evac_k)
    with em.scope():
        wt = em.sbuf("wt4", [128, 4, KC, 128], BF16)
        for half in range(2):
            def evac_v(bi, ps, half=half):
                em.copy("act", RV.v((SL, bi, slice(512 * half, 512 * half + 512)), keys=((bi, half),)), ps.v())
            proj_tm(cx, XB, [256 + 128 * b for b in range(8)], [win1[4 + 4 * half + c] for c in range(4)], wt, evac_v)
    ktok = em.sbuf("ktok", [128, 4, 8, 128], BF16)
    bi_ = 0
    for h in range(4):
        for n4 in range(2):
            pb = cx.ps[6 + (bi_ % 2)]
            bi_ += 1
            for j in range(4):
                n = 4 * n4 + j
                xti = 1 if n < 4 else 2
                em.transpose(View(pb, pb.h.bitcast(BF16)[:, 128 * j:128 * j + 128], (None,)),
                             RK.v((SL, h, slice(128 * n, 128 * n + 128)), keys=((h, xti),)), cx.ident.v(), inc=(j == 3))
            em.copy("act", View(ktok, ktok.h[:, h, 4 * n4:4 * n4 + 4, :].rearrange("p a b -> p (a b)"), tuple((h, 4 * n4 + j) for j in range(4))),
                    psbf(pb, 512))
    vz = [em.sbuf("vz%d" % i, [128, 256], BF16) for i in range(16)]
    lsb = [em.sbuf("lsb%d" % i, [128, 256], F32) for i in range(2)]
    cnt = 0
    for d_ in range(2):
        for h in range(4):
            ps = cx.ps[cnt % 2]
            vs = []
            for n in range(8):
                v_ = vz[(cnt * 8 + n) % 16]
                em.ts("dve", v_.v(), RV.v((SL, n, slice(256 * h, 256 * h + 256)), keys=((n, h // 2),)),
                      wts.v((SL, d_, h, slice(n, n + 1))), None, ALU.mult)
                vs.append(v_)
            for n in range(8):
                em.mm(ps.v((SL, slice(0, 256))), ktok.v((SL, h, n), keys=((h, n),)), vs[n].v(), start=(n == 0), stop=(n == 7), inc=True)
            em.copy("act", lsb[cnt % 2].v(), ps.v((SL, slice(0, 256))))
            em.dma("sp", View(lfb, lfb.h[d_, h], ((d_, h),)), lsb[cnt % 2].v(), semof=lsb[cnt % 2].v())
            cnt += 1


def lay_w(w):
    n = w.shape[1] // 128
    return np.ascontiguousarray(w.reshape(KC, 128, n, 128).transpose(2, 1, 0, 3))


def rope_tables(pos):
    pos = np.asarray(pos)
    d = np.arange(128)
    a = d // 64
    f = d % 32
    p = (d % 64) // 32
    inv = 10000.0 ** (-(2.0 * f) / 64.0)
    rows = (pos // 64).astype(np.float64)
    cols = (pos % 64).astype(np.float64)
    coord = np.where(a[:, None] == 0, rows[None, :], cols[None, :])
    ang = (coord.astype(np.float32) * inv[:, None].astype(np.float32)).astype(np.float32)
    cos = np.cos(ang.astype(np.float64))
    sin = np.sin(ang.astype(np.float64)) * np.where(p == 0, -1.0, 1.0)[:, None]
    return cos.astype(np.float32), sin.astype(np.float32)


def perm_matrix():
    m = np.arange(128)
    partner = np.where((m % 64) < 32, m + 32, m - 32)
    P = np.zeros((128, 128), np.float32)
    P[partner, m] = 1.0
    return P


def inputs_B(core, hx_a, hc_a, mods0, mods1, inp):
    c = core
    S = hx_a.shape[0]
    lo, hi = LAT * c, LAT * (c + 1)
    hin = np.concatenate([hc_a.T, hx_a[lo:hi].T], axis=1)
    hl = hx_a[lo - 128:lo].T if c > 0 else np.zeros((D, 128), np.float32)
    hr = hx_a[hi:hi + 128].T if hi + 128 <= S else np.zeros((D, 128), np.float32)
    pos = np.concatenate([np.arange(lo, hi), np.arange(lo - 128, lo), np.arange(hi, hi + 128)])
    cos, sin = rope_tables(np.clip(pos, 0, 8191))
    kq = np.arange(128)
    mL = (kq[:, None] >= kq[None, :]).astype(np.float32)
    mR = (kq[:, None] <= kq[None, :]).astype(np.float32)
    z = np.zeros_like(mL)
    masks = np.stack([mL if c > 0 else z, mL, mR, mR if c < NCORE - 1 else z], axis=1)
    cw, cb = inp["ev_conv_w"][0], inp["ev_conv_b"][0]
    convp = np.stack([cw[0], cw[1], cw[2], cb], axis=-1).reshape(8, 128, 4).transpose(1, 0, 2)
    j = np.arange(128)[:, None]
    n = np.arange(8)[None, :]
    expo = np.stack([1023.0 - (128 * n + j), 128.0 * n + j], axis=1).astype(np.float32)
    m0 = np.stack([mods0[0][3:9], mods0[1][3:9]])
    m1 = np.stack([mods1[0][0:6], mods1[1][0:6]])
    wl1 = lay_w(inp["od_w_in"][0])
    return {
        "hin": np.ascontiguousarray(hin), "hhalo": np.ascontiguousarray(np.concatenate([hl, hr], axis=1)),
        "mod0": lay_vec(m0), "nw0": lay_vec(inp["norm_w"][0][1:3]),
        "mod1": lay_vec(m1), "nw1": lay_vec(inp["norm_w"][1][0:2]),
        "win": lay_w(inp["ev_w_in"][0]), "wout": lay_w(inp["ev_w_out"][0]),
        "wib": lay_wi(inp["ffn_b_wi"][0]), "wob": lay_wo(inp["ffn_b_wo"][0]),
        "wia": lay_wi(inp["ffn_a_wi"][1]), "woa": lay_wo(inp["ffn_a_wo"][1]),
        "cos": cos, "sin": sin, "pm": perm_matrix(), "masks": np.ascontiguousarray(masks),
        "sink": np.ascontiguousarray(inp["ev_sink"][0]), "convp": np.ascontiguousarray(convp),
        "edge": np.tile(np.array([[float(c > 0), float(c < NCORE - 1)]], np.float32), (128, 1)),
        "win1": np.ascontiguousarray(wl1[4:16]),
        "dec": np.stack([inp["od_decay_f"][0], inp["od_decay_b"][0]]).astype(np.float32),
        "expo": np.ascontiguousarray(expo),
    }


TTC = [(0, 256), (256, 512), (768, 512), (1280, 256), (1536, 256)]
TTL = [(0, 512), (512, 512)]
NA_PAIRS = [(j, s) for j in range(8) for s in ([-2, -1, 0, 1, 2] + ([3] if j == 0 else []) + ([-3] if j == 7 else []))]


def na_blk_col(b):
    if b < 0:
        return 1280 + 128 * (b + 2)
    if b < 8:
        return 256 + 128 * b
    return 1536 + 128 * (b - 8)


def build_C(upto=3):
    nc = bass.Bass("TRN2", target_bir_lowering=False)
    cx = Ctx(nc)
    em = cx.em
    din = lambda name, shape: nc.dram_tensor(name, list(shape), F32, kind="ExternalInput")
    hin = din("hin", [D, T0])
    hhalo = din("hhalo", [D, 512])
    mod1 = din("mod1", [128, 2, 6, KC])
    nw1 = din("nw1", [128, 2, KC])
    fnw = din("fnw", [128, KC])
    win = din("win", [48, 128, KC, 128])
    cosd, sind = din("cos", [128, LAT]), din("sin", [128, LAT])
    pmd = din("pm", [128, 128])
    decd = din("dec", [2, 4])
    lall = din("lall", [2, 8, 128, 4, 256])
    coefd = din("coef", [128, 2, 2, 9])
    rcd = din("rconst", [128, 6, 128])
    zcd = din("zconst", [128, 2, 3])
    gnwd = din("gnw", [128, 8])
    nab = din("nab", [len(NA_PAIRS), 2, 128, 4, 128]) if upto >= 1 else None
    yout = em.dram("yout", [D, LAT], F32, kind="ExternalOutput")
    norm_scratch(cx)
    make_ident(cx)
    XB = em.sbuf("XB", [128, KC, LAT], BF16)
    XB.tt = TTL
    m1, A1 = prep_mod(cx, mod1[:, :, :, :], nw1[:, :, :], 2)
    hin_v = hin.ap().rearrange("(k p) t -> p k t", p=128)
    hh_v = hhalo.ap().rearrange("(k p) t -> p k t", p=128)
    vec_of = [1, 0, 0, 0, 0]

    with em.scope():
        xm = em.sbuf("xm", [128, KC, 1792], BF16)
        xm.tt = TTC
        with em.scope():
            hst = [em.sbuf("hst%d" % i, [128, KC, 512], F32) for i in range(2)]
            for ti, (lo, n) in enumerate(TTC):
                hs = hst[ti % 2]
                hs.tt = [(0, n)]
                with em.dma_group("hst%d" % ti):
                    for k in range(KC):
                        src = hin_v[:, k, lo:lo + n] if ti < 3 else hh_v[:, k, lo - 1280:lo - 1280 + n]
                        em.dma("sp", kt(hs, k, 0), src)
                rmsnorm_mod(cx, hs, xm, [(0, ti, (lambda k, v=vec_of[ti]: A1.v((SL, v, 0, slice(k, k + 1)))),
                                          (lambda k, v=vec_of[ti]: m1.v((SL, v, 0, slice(k, k + 1)))))], 6)
        with em.scope():
            RQ = em.sbuf("RQ", [128, 4, LAT], BF16)
            RK = em.sbuf("RK", [128, 4, T0], BF16)
            RV = em.sbuf("RV", [128, 10, 1024], BF16)
            with em.scope():
                rope_setup(cx, cosd[:, :], sind[:, :], pmd[:, :], LAT)
                ring = Ring(em, "wringr", [KC, 128], BF16, 4)

                def evac_q(ci, xti, ps, n):
                    lo, _ = TTC[xti]
                    rope_evac(cx, ps, n, RQ.v((SL, ci, slice(lo - 256, lo - 256 + n)), keys=((ci, xti),)),
                              cx.COS.v((SL, slice(lo - 256, lo - 256 + n))), cx.SIN.v((SL, slice(lo - 256, lo - 256 + n))))
                proj_fm(cx, xm, [1, 2], [win[c] for c in range(4)], ring, evac_q)

                def evac_k(ci, xti, ps, n):
                    lo, _ = TTC[xti]
                    out = RK.v((SL, ci, slice(lo, lo + n)), keys=((ci, xti),))
                    if xti == 0:
                        em.copy("act", out, ps.v((SL, slice(0, n))))
                    else:
                        rope_evac(cx, ps, n, out, cx.COS.v((SL, slice(lo - 256, lo - 256 + n))), cx.SIN.v((SL, slice(lo - 256, lo - 256 + n))))
                proj_fm(cx, xm, [0, 1, 2], [win[4 + c] for c in range(4)], ring, evac_k)
                wt = em.sbuf("wt4", [128, 4, KC, 128], BF16)
                for half in range(2):
                    def evac_v(bi, ps, half=half):
                        em.copy("act", RV.v((SL, bi, slice(512 * half, 512 * half + 512)), keys=((bi, half),)), ps.v())
                    proj_tm(cx, xm, [128 * b for b in range(10)], [win[8 + 4 * half + c] for c in range(4)], wt, evac_v)
            retention_C(cx, XB, RQ, RK, RV, decd, lall, coefd, rcd, zcd, gnwd)
        if upto < 1:
            return _dump_C(cx, nc, XB, None, yout)
        with em.scope():
            ring = Ring(em, "wringg", [KC, 128], BF16, 4)
            sgt = [em.sbuf("sgt%d" % i, [128, 512], F32) for i in range(2)]
            gcnt = [0]

            def evac_g(ci, xti, ps, n):
                t_ = sgt[gcnt[0] % 2]
                gcnt[0] += 1
                em.act(t_.v(), ps.v(), AF.Silu)
                em.tt("dve", kt(XB, ci, xti - 1), kt(XB, ci, xti - 1), t_.v(), ALU.mult)
            proj_fm(cx, xm, [1, 2], [win[16 + c] for c in range(8)], ring, evac_g)
        with em.scope():
            NQ = em.sbuf("NQ", [128, 8, LAT], BF16)
            NK = em.sbuf("NK", [128, 8, 1792], BF16)
            NV = em.sbuf("NV", [128, 14, 1024], BF16)
            with em.scope():
                ring = Ring(em, "wringn", [KC, 128], BF16, 4)

                def evac_nq(ci, xti, ps, n):
                    lo, _ = TTC[xti]
                    em.copy("act", NQ.v((SL, ci, slice(lo - 256, lo - 256 + n)), keys=((ci, xti),)), ps.v((SL, slice(0, n))))
                proj_fm(cx, xm, [1, 2], [win[24 + c] for c in range(8)], ring, evac_nq)
                ecnt = [0]

                def evac_nk(ci, xti, ps, n):
                    lo, _ = TTC[xti]
                    eng = "act" if ecnt[0] % 2 == 0 else "dve"
                    ecnt[0] += 1
                    em.copy(eng, NK.v((SL, ci, slice(lo, lo + n)), keys=((ci, xti),)), ps.v((SL, slice(0, n))))
                proj_fm(cx, xm, [0, 1, 2, 3, 4], [win[32 + c] for c in range(8)], ring, evac_nk)
                wt = em.sbuf("wt4n", [128, 4, KC, 128], BF16)
                for half in range(2):
                    def evac_nv(bi, ps, half=half):
                        em.copy("act", NV.v((SL, bi, slice(512 * half, 512 * half + 512)), keys=((bi, half),)), ps.v())
                    proj_tm(cx, xm, [128 * b for b in range(14)], [win[40 + 4 * half + c] for c in range(4)], wt, evac_nv)
            with em.scope():
                attn_setup(cx)
                bring = Ring(em, "nabr", [4, 128], F32, 8)
                pair_idx = {p: i for i, p in enumerate(NA_PAIRS)}

                def tile_of(col):
                    for ti, (lo, n) in enumerate(TTC):
                        if lo <= col < lo + n:
                            return ti
                for j in range(8):
                    for hg in range(2):
                        rel = [s for (jj, s) in NA_PAIRS if jj == j]
                        slots = []
                        for s_ in rel:
                            b = j + s_
                            col = na_blk_col(b)
                            bs = bring.load(nab[pair_idx[(j, s_)], hg], eng="sp")
                            slots.append(dict(
                                k=(lambda hh, col=col: NK.v((SL, 4 * hg + hh, slice(col, col + 128)), keys=((4 * hg + hh, tile_of(col)),))),
                                v=(lambda hh, col=col: NV.v((SL, col // 128, slice(128 * (4 * hg + hh), 128 * (4 * hg + hh) + 128)), keys=((col // 128, hg),))),
                                bias=bring.view(bs)))
                        for cb in range(2):
                            col = 128 * cb
                            slots.append(dict(
                                k=(lambda hh, col=col: NK.v((SL, 4 * hg + hh, slice(col, col + 128)), keys=((4 * hg + hh, 0),))),
                                v=(lambda hh, col=col: NV.v((SL, col // 128, slice(128 * (4 * hg + hh), 128 * (4 * hg + hh) + 128)), keys=((col // 128, hg),)))))
                        xti = 1 if j < 4 else 2
                        qh = lambda hh: NQ.v((SL, 4 * hg + hh, slice(128 * j, 128 * j + 128)), keys=((4 * hg + hh, xti),))
                        out4 = XB.v((SL, slice(8 + 4 * hg, 12 + 4 * hg), slice(128 * j, 128 * j + 128)),
                                    keys=tuple((8 + 4 * hg + hh, j // 4) for hh in range(4)))
                        attn_tile(cx, None, qh, slots, ATT_SCALE, None, out4, False)

    if upto < 2:
        return _dump_C(cx, nc, XB, None, yout)
    wout = din("wout", [16, 128, KC, 128])
    H = em.sbuf("H", [128, KC, LAT], F32)
    H.tt = TTL
    with em.dma_group("Hld"):
        for k in range(KC):
            em.dma("sp", H.v((SL, k), keys=tuple((k, ti) for ti in range(2))), hin_v[:, k, 256:256 + LAT])
    with em.scope():
        ring = Ring(em, "wring2", [KC, 128], BF16, 4)

        def evac_o(ci, xti, ps, n):
            em.stt("dve", kt(H, ci, xti), ps.v((SL, slice(0, n))), m1.v((SL, 0, 2, slice(ci, ci + 1))), kt(H, ci, xti), ALU.mult, ALU.add)
        proj_fm(cx, XB, [0, 1], [wout[d] for d in range(16)], ring, evac_o)
    if upto < 3:
        return _dump_C(cx, nc, XB, H, yout)
    wib, wob = din("wib", [FC, 128, KC, 2, 128]), din("wob", [FC, 128, D])
    gh = em.sbuf("gh", [128, 2, KC], F32)
    em.ts("dve", gh.v(), m1.v((SL, SL, 5)), 0.5, None, ALU.mult)
    rmsnorm_mod(cx, H, XB, [(ti, ti, (lambda k: A1.v((SL, 0, 1, slice(k, k + 1)))),
                             (lambda k: m1.v((SL, 0, 3, slice(k, k + 1))))) for ti in range(2)], 6)
    ffn(cx, H, XB, [(ti, ti, 0) for ti in range(2)], wib, wob, lambda vec, d: gh.v((SL, vec, slice(d, d + 1))))
    fw_ = em.sbuf("fnw", [128, KC], F32)
    em.dma("sp", fw_.v(), fnw[:, :])
    yo_v = yout.h.ap().rearrange("(k p) t -> p k t", p=128)
    with em.scope():
        o32 = [em.sbuf("o32%d" % i, [128, 512], F32) for i in range(3)]
        oc = [0]

        def out_fn(k, ti, tmp_view, n):
            lo, _ = TTL[ti]
            o = o32[oc[0] % 3]
            oc[0] += 1
            em.act(o.v((SL, slice(0, n))), tmp_view, AF.Identity, scale=fw_.v((SL, slice(k, k + 1))))
            em.dma("sp", View(yout, yo_v[:, k, lo:lo + n], ((k, ti),)), o.v((SL, slice(0, n))), semof=o.v())
        rmsnorm_mod(cx, H, XB, [(ti, ti, None, None) for ti in range(2)], 6, out_fn=out_fn)
    em.finish([View(yout, yout.h[:, :], tuple((k, ti) for k in range(KC) for ti in range(2)))])
    return nc


def retention_C(cx, XB, RQ, RK, RV, decd, lall, coefd, rcd, zcd, gnwd):
    em = cx.em
    AX = mybir.AxisListType.X
    lg, lns = load_lg(cx, decd)
    rc = em.sbuf("rc", [128, 6, 128], F32)
    zc = em.sbuf("zc", [128, 2, 3], F32)
    gnw = em.sbuf("gnw", [128, 8], F32)
    cf = em.sbuf("cf", [128, 2, 2, 9], F32)
    em.dma("sp", rc.v(), rcd[:, :, :])
    em.dma("sp", zc.v(), zcd[:, :, :])
    em.dma("sp", gnw.v(), gnwd[:, :])
    em.dma("sp", cf.v(), coefd[:, :, :, :])
    MT = em.sbuf("MT", [128, 4, 128], F32)
    XIF = em.sbuf("XIF", [128, 4, 128], F32)
    XIB = em.sbuf("XIB", [128, 4, 128], F32)
    tmpm = em.sbuf("tmpm", [128, 128], F32)
    zeta = em.sbuf("zeta", [128, 2, 4], F32)
    cw = em.sbuf("cw", [128, 2, 4, 2], F32)
    gC = em.sbuf("gC", [128, 8], F32)
    coef = em.sbuf("coef", [128, 2, 4, 9], F32)
    lgc = lambda d_, h: lg.v((SL, slice(4 * d_ + h, 4 * d_ + h + 1)))
    for h in range(4):
        em.act(MT.v((SL, h)), rc.v((SL, 0)), AF.Exp, bias=lns.v(), scale=lgc(0, h))
        em.tt("dve", MT.v((SL, h)), MT.v((SL, h)), rc.v((SL, 1)), ALU.mult)
        em.act(tmpm.v(), rc.v((SL, 2)), AF.Exp, bias=lns.v(), scale=lgc(1, h))
        em.tt("dve", tmpm.v(), tmpm.v(), rc.v((SL, 3)), ALU.mult)
        em.tt("dve", MT.v((SL, h)), MT.v((SL, h)), tmpm.v(), ALU.add)
        em.act(XIF.v((SL, h)), rc.v((SL, 4)), AF.Exp, scale=lgc(0, h))
        em.act(XIB.v((SL, h)), rc.v((SL, 5)), AF.Exp, scale=lgc(1, h))
    for d_ in range(2):
        for h in range(4):
            em.act(zeta.v((SL, d_, slice(h, h + 1))), zc.v((SL, d_, slice(0, 1))), AF.Exp, bias=lns.v(), scale=lgc(d_, h))
            em.act(cw.v((SL, d_, h)), zc.v((SL, d_, slice(1, 3))), AF.Exp, bias=lns.v(), scale=lgc(d_, h))
            em.act(coef.v((SL, d_, h)), cf.v((SL, d_, 0)), AF.Exp, scale=lgc(d_, h))
            em.tt("dve", coef.v((SL, d_, h)), coef.v((SL, d_, h)), cf.v((SL, d_, 1)), ALU.mult)
    em.act(gC.v(), lg.v(), AF.Exp, scale=128.0)
    ktok = em.sbuf("ktok", [128, 4, 10, 128], BF16)
    for h in range(4):
        for b in range(10):
            pb = cx.ps[6 + (b % 2)]
            xti = 0 if b < 2 else (1 if b < 6 else 2)
            em.transpose(psbf(pb, 128), RK.v((SL, h, slice(128 * b, 128 * b + 128)), keys=((h, xti),)), cx.ident.v())
            em.copy("act", ktok.v((SL, h, b), keys=((h, b),)), psbf(pb, 128))
    vz = [em.sbuf("vz%d" % i, [128, 256], BF16) for i in range(8)]
    vzc = [0]

    def kv_mm(out_view, h, b, wcol):
        v_ = vz[vzc[0] % 8]
        vzc[0] += 1
        em.ts("dve", v_.v(), RV.v((SL, b, slice(256 * h, 256 * h + 256)), keys=((b, h // 2),)), wcol, None, ALU.mult)
        return v_

    S0 = em.sbuf("S0", [128, 2, 4, 256], F32)
    cnt = 0
    for d_ in range(2):
        for h in range(4):
            ps = cx.ps[cnt % 2]
            cnt += 1
            for b in range(2):
                v_ = kv_mm(None, h, b, cw.v((SL, d_, h, slice(b, b + 1))))
                em.mm(ps.v((SL, slice(0, 256))), ktok.v((SL, h, b), keys=((h, b),)), v_.v(), start=(b == 0), stop=(b == 1), inc=True)
            em.ts("dve", S0.v((SL, d_, h), keys=((d_, h),)), ps.v((SL, slice(0, 256))), coef.v((SL, d_, h, slice(8, 9))), None, ALU.mult)
    with em.scope():
        lr = Ring(em, "lring", [4, 256], F32, 3)
        for d_ in range(2):
            for r in range(8):
                s_ = lr.load(lall[d_, r], eng="sp")
                for h in range(4):
                    em.stt("dve", S0.v((SL, d_, h), keys=((d_, h),)), lr.view(s_, (h,)), coef.v((SL, d_, h, slice(r, r + 1))),
                           S0.v((SL, d_, h), keys=((d_, h),)), ALU.mult, ALU.add)
    SBst = em.sbuf("SBst", [128, 8, 4, 256], BF16)

    def state_update(d_, n):
        vs, ovs = [], []
        for h in range(4):
            vs.append(kv_mm(None, h, 2 + n, zeta.v((SL, d_, slice(h, h + 1)))))
        for h in range(4):
            bank = cx.ps[1] if h < 2 else cx.ps[7]
            ov = bank.v((SL, slice(256 * (h % 2), 256 * (h % 2) + 256)))
            em.mm(ov, ktok.v((SL, h, 2 + n), keys=((h, 2 + n),)), vs[h].v(), inc=True)
            ovs.append(ov)
        for h in range(4):
            em.stt("dve", S0.v((SL, d_, h), keys=((d_, h),)), S0.v((SL, d_, h), keys=((d_, h),)), gC.v((SL, slice(4 * d_ + h, 4 * d_ + h + 1))),
                   ovs[h], ALU.mult, ALU.add)

    for n in range(7, -1, -1):
        em.copy("act", SBst.v((SL, n), keys=(n,)), S0.v((SL, 1), keys=tuple((1, h) for h in range(4))))
        if n > 0:
            state_update(1, n)
    Sfb = [em.sbuf("Sfb%d" % i, [128, 4, 256], BF16) for i in range(2)]
    PT = [em.sbuf("PT%d" % i, [128, 4, 128], BF16) for i in range(2)]
    qf = [em.sbuf("qf%d" % i, [128, 4, 128], BF16) for i in range(2)]
    qb = [em.sbuf("qb%d" % i, [128, 4, 128], BF16) for i in range(2)]
    ysb = [em.sbuf("ysb%d" % i, [128, 4, 256], F32) for i in range(2)]
    sq = em.sbuf("sq", [128, 4, 256], F32)
    yn = [em.sbuf("yn%d" % i, [128, 4, 256], BF16) for i in range(2)]
    st = [em.sbuf("st%d" % i, [128, 5, 4], F32) for i in range(2)]

    def v3(t, a, b):
        return View(t, t.h[:].rearrange("p (a b) -> p a b", a=a), (None,))

    def head(n):
        i = n % 2
        xti = 1 if n < 4 else 2
        qcols = slice(128 * n, 128 * n + 128)
        em.copy("act", Sfb[i].v(), S0.v((SL, 0), keys=tuple((0, h) for h in range(4))))
        A = cx.ps[0]
        for h in range(4):
            em.mm(A.v((SL, slice(128 * h, 128 * h + 128))), RK.v((SL, h, slice(256 + 128 * n, 256 + 128 * n + 128)), keys=((h, xti),)),
                  RQ.v((SL, h, qcols), keys=((h, xti),)), inc=(h == 3))
        em.tt("dve", PT[i].v(), v3(A, 4, 128), MT.v(), ALU.mult)
        rq4 = RQ.v((SL, SL, qcols), keys=tuple((h, xti) for h in range(4)))
        em.tt("pool", qf[i].v(), rq4, XIF.v(), ALU.mult)
        em.tt("pool", qb[i].v(), rq4, XIB.v(), ALU.mult)
        yb = (cx.ps[2], cx.ps[3]) if i == 0 else (cx.ps[4], cx.ps[5])
        for h in range(4):
            yv = yb[h // 2].v((SL, slice(256 * (h % 2), 256 * (h % 2) + 256)))
            rvv = RV.v((SL, 2 + n, slice(256 * h, 256 * h + 256)), keys=((2 + n, h // 2),))
            em.mm(yv, View(PT[i], PT[i].h[:, h, :], (None,)), rvv, start=True, stop=False)
            em.mm(yv, View(qf[i], qf[i].h[:, h, :], (None,)), View(Sfb[i], Sfb[i].h[:, h, :], (None,)), start=False, stop=False)
            em.mm(yv, View(qb[i], qb[i].h[:, h, :], (None,)), SBst.v((SL, n, h), keys=(n,)), start=False, stop=True)

    def tail(n):
        i = n % 2
        qcols = slice(128 * n, 128 * n + 128)
        yb = (cx.ps[2], cx.ps[3]) if i == 0 else (cx.ps[4], cx.ps[5])
        y_ = ysb[i]
        for half in range(2):
            em.copy("act", View(y_, y_.h[:, 2 * half:2 * half + 2, :], (None,)), v3(yb[half], 2, 256))
        s_ = st[i]
        em.op("dve", lambda e, o=s_.h[:, 0, :], a=y_.h[:]: e.tensor_reduce(o, a, AX, ALU.add), [y_.v()], [s_.v()])
        em.tt("pool", sq.v(), y_.v(), y_.v(), ALU.mult)
        em.op("dve", lambda e, o=s_.h[:, 1, :], a=sq.h[:]: e.tensor_reduce(o, a, AX, ALU.add), [sq.v()], [s_.v()])
        em.ts("dve", s_.v((SL, 2)), s_.v((SL, 0)), 1.0 / 256, None, ALU.mult)
        em.tt("dve", s_.v((SL, 3)), s_.v((SL, 2)), s_.v((SL, 2)), ALU.mult)
        em.stt("dve", s_.v((SL, 3)), s_.v((SL, 1)), 1.0 / 256, s_.v((SL, 3)), ALU.mult, ALU.subtract)
        em.act(s_.v((SL, 4)), s_.v((SL, 3)), AF.Sqrt, bias=cx.epsc.v())
        em.recip(s_.v((SL, 4)), s_.v((SL, 4)))
        for h in range(4):
            em.ts("dve", View(yn[i], yn[i].h[:, h, :], (None,)), View(y_, y_.h[:, h, :], (None,)),
                  s_.v((SL, 2, slice(h, h + 1))), s_.v((SL, 4, slice(h, h + 1))), ALU.subtract, ALU.mult)
        trb = cx.ps[6]
        ynf = yn[i].h[:].rearrange("p a b -> p (a b)")
        for c in range(8):
            em.transpose(View(trb, trb.h.bitcast(BF16)[:, 128 * c:128 * c + 128], (None,)),
                         View(yn[i], ynf[:, 128 * c:128 * c + 128], (None,)), cx.ident.v(), inc=(c == 7))
        trv = View(trb, trb.h.bitcast(BF16)[:, 0:1024].rearrange("p (c q) -> p c q", c=8), (None,))
        gb = View(gnw, gnw.h[:, :].unsqueeze(2).broadcast_to([128, 8, 128]), (None,))
        em.tt("dve", XB.v((SL, slice(0, 8), qcols), keys=tuple((c, n // 4) for c in range(8))), trv, gb, ALU.mult)

    head(0)
    for n in range(8):
        if n < 7:
            state_update(0, n)
            head(n + 1)
        tail(n)


MCH = 18


def build_M():
    nc = bass.Bass("TRN2", target_bir_lowering=False)
    cx = Ctx(nc)
    em = cx.em
    adaw = nc.dram_tensor("adaw", [2 * MCH, 128, KC, 128], F32, kind="ExternalInput")
    adab = nc.dram_tensor("adab", [128, 2 * MCH], F32, kind="ExternalInput")
    cvec = nc.dram_tensor("cvec", [128, KC, 2], F32, kind="ExternalInput")
    mout = em.dram("mout", [128, 2 * MCH, 2], F32, kind="ExternalOutput")
    cv = em.sbuf("cv", [128, KC, 2], F32)
    cb = em.sbuf("cb", [128, KC, 2], BF16)
    ab = em.sbuf("ab", [128, 2 * MCH], F32)
    mo = em.sbuf("mo", [128, 2 * MCH, 2], F32)
    em.dma("sp", cv.v(), cvec[:, :, :])
    em.dma("sp", ab.v(), adab[:, :])
    em.act(cb.v(), cv.v(), AF.Silu)
    ring = Ring(em, "mring", [KC, 128], BF16, 4)
    for ci, ws in stream(ring, [adaw[i] for i in range(2 * MCH)], 3):
        ps = cx.ps[ci % 4]
        for k in range(KC):
            em.mm(ps.v((SL, slice(0, 2))), ring.view(ws, (k,)), cb.v((SL, k)), start=(k == 0), stop=(k == KC - 1))
        em.ts("dve", mo.v((SL, ci)), ps.v((SL, slice(0, 2))), ab.v((SL, slice(ci, ci + 1))), None, ALU.add)
    em.dma("sp", mout.v(), mo.v())
    em.finish([mout.v()])
    return nc


def run_M(inp):
    nc = build_M()
    aw, ab_ = inp["ada_w"], inp["ada_b"]
    cvec = np.ascontiguousarray(np.stack([lay_vec(inp["c"][0]), lay_vec(inp["c_ctx"])], axis=-1))
    maps = []
    ncol = MCH * 128
    for c in range(NCORE):
        wl = [lay_w(np.ascontiguousarray(aw[l][:, c * ncol:(c + 1) * ncol])) for l in range(2)]
        bl = [ab_[l][c * ncol:(c + 1) * ncol].reshape(MCH, 128).T for l in range(2)]
        maps.append({"adaw": np.ascontiguousarray(np.concatenate(wl, 0)), "adab": np.ascontiguousarray(np.concatenate(bl, 1)), "cvec": cvec})
    res = run_bass_kernel_spmd(nc, maps, core_ids=list(range(NCORE)))
    full = np.zeros((2, 2, 9 * D), np.float32)
    for c in range(NCORE):
        mo = res.results[c]["mout"]
        for l in range(2):
            blk = mo[:, l * MCH:(l + 1) * MCH, :]
            full[l, :, c * ncol:(c + 1) * ncol] = blk.transpose(2, 1, 0).reshape(2, ncol)
    return full.reshape(2, 2, 9, D)


def na_bias_tables(core, rpb):
    out = np.empty((len(NA_PAIRS), 2, 128, 4, 128), np.float32)
    loc = np.arange(128)
    lr, lc = loc // 64, loc % 64
    for pi, (j, s) in enumerate(NA_PAIRS):
        bq = 8 * core + j
        bk = bq + s
        r = (2 * bq + lr)[None, :]
        qc = lc[None, :]
        kr = (2 * bk + lr)[:, None]
        kc = lc[:, None]
        rs = np.clip(r - 4, 0, 120)
        cs = np.clip(qc - 8, 0, 48)
        valid = (kr >= rs) & (kr < rs + 8) & (kc >= cs) & (kc < cs + 16) & (bk >= 0) & (bk < 64)
        dr = np.clip(kr - r + 7, 0, 14)
        dc = np.clip(kc - qc + 15, 0, 30)
        b = rpb[:, dr, dc]
        b = np.where(valid[None], b, np.float32(-30000.0)).astype(np.float32)
        out[pi] = b.reshape(2, 4, 128, 128).transpose(0, 2, 1, 3)
    return out


def inputs_C(core, houts, lfbs, mods1, inp, wcache):
    c = core
    hin = houts[c]
    up = houts[c - 1][:, 256 + 768:256 + 1024] if c > 0 else np.zeros((D, 256), np.float32)
    dn = houts[c + 1][:, 256:512] if c < NCORE - 1 else np.zeros((D, 256), np.float32)
    cos, sin = rope_tables(np.arange(LAT * c, LAT * (c + 1)))
    lall = np.ascontiguousarray(np.stack(lfbs, axis=1).transpose(0, 1, 3, 2, 4))
    coef = np.zeros((2, 2, 9), np.float32)
    for r in range(8):
        if r < c:
            coef[0, 0, r], coef[0, 1, r] = 1024.0 * (c - 1 - r), 1.0
        if r > c:
            coef[1, 0, r], coef[1, 1, r] = 1024.0 * (r - c - 1), 1.0
    coef[0, 0, 8], coef[0, 1, 8] = 1024.0 * c, 1.0
    coef[1, 0, 8], coef[1, 1, 8] = 1024.0 * (7 - c), 1.0
    jj = np.arange(128, dtype=np.float32)
    J, I = jj[:, None], jj[None, :]
    rconst = np.stack([np.maximum(I - J, 0), (I >= J).astype(np.float32), np.maximum(J - I, 0), (J > I).astype(np.float32),
                       np.broadcast_to(I + 1, (128, 128)), np.broadcast_to(128 - I, (128, 128))], axis=1).astype(np.float32)
    zconst = np.stack([np.stack([127 - jj, 255 - jj, 127 - jj], -1), np.stack([jj, jj, 128 + jj], -1)], axis=1).astype(np.float32)
    m1 = np.stack([mods1[0][3:9], mods1[1][3:9]])
    d = {
        "hin": np.ascontiguousarray(hin), "hhalo": np.ascontiguousarray(np.concatenate([up, dn], axis=1)),
        "mod1": lay_vec(m1), "nw1": lay_vec(inp["norm_w"][1][1:3]), "fnw": lay_vec(inp["final_norm_w"]),
        "cos": cos, "sin": sin, "pm": perm_matrix(),
        "dec": np.stack([inp["od_decay_f"][0], inp["od_decay_b"][0]]).astype(np.float32),
        "lall": lall, "coef": np.ascontiguousarray(np.broadcast_to(coef[None], (128, 2, 2, 9))),
        "rconst": np.ascontiguousarray(rconst), "zconst": np.ascontiguousarray(zconst),
        "gnw": np.ascontiguousarray(inp["od_gn_w"][0].reshape(8, 128).T),
        "nab": na_bias_tables(c, inp["od_rpb"][0]),
    }
    d.update(wcache)
    return d


def kernel(**inp):
    inp = {k: np.asarray(v) for k, v in inp.items()}
    ids = list(range(NCORE))
    mods = run_M(inp)
    x, ctx = inp["x"][0], inp["ctx"][0]
    ncA = build_A()
    wA = {"wi": lay_wi(inp["ffn_a_wi"][0]), "wo": lay_wo(inp["ffn_a_wo"][0]), "mod": lay_vec(mods[0][:, 0:3]), "nw": lay_vec(inp["norm_w"][0][0:1])}
    mapsA = [dict(wA, xT=np.ascontiguousarray(np.concatenate([ctx[CO * c:CO * (c + 1)].T, x[LAT * c:LAT * (c + 1)].T], axis=1))) for c in ids]
    resA = run_bass_kernel_spmd(ncA, mapsA, core_ids=ids)
    hA = [resA.results[c]["hout"] for c in ids]
    del mapsA, wA
    hc_a = np.concatenate([hA[c][:, :CO].T for c in ids], axis=0)
    hx_a = np.concatenate([hA[c][:, CO:].T for c in ids], axis=0)
    ncB = build_B()
    base = inputs_B(0, hx_a, hc_a, (mods[0][0], mods[0][1]), (mods[1][0], mods[1][1]), inp)
    mapsB = [dict(base, **inputs_B_core(c, hx_a, hc_a)) for c in ids]
    resB = run_bass_kernel_spmd(ncB, mapsB, core_ids=ids)
    hB = [resB.results[c]["hout"] for c in ids]
    lfbs = [resB.results[c]["lfb"] for c in ids]
    hc_1a = np.concatenate([hB[c][:, CB:CTX].T for c in ids], axis=0)
    hB = [np.concatenate([hc_1a.T, hB[c][:, CTX:]], axis=1) for c in ids]
    del mapsB, base
    ncC = build_C()
    wC = {"win": lay_w(inp["od_w_in"][0]), "wout": lay_w(inp["od_w_out"][0]), "wib": lay_wi(inp["ffn_b_wi"][1]), "wob": lay_wo(inp["ffn_b_wo"][1])}
    mapsC = [inputs_C(c, hB, lfbs, (mods[1][0], mods[1][1]), inp, wC) for c in ids]
    resC = run_bass_kernel_spmd(ncC, mapsC, core_ids=ids)
    out = np.concatenate([resC.results[c]["yout"].T for c in ids], axis=0)
    return np.ascontiguousarray(out[None].astype(np.float32))


def inputs_B_core(core, hx_a, hc_a):
    c = core
    S = hx_a.shape[0]
    lo, hi = LAT * c, LAT * (c + 1)
    hin = np.concatenate([np.roll(hc_a, CB - CO * c, axis=0).T, hx_a[lo:hi].T], axis=1)
    hl = hx_a[lo - 128:lo].T if c > 0 else np.zeros((D, 128), np.float32)
    hr = hx_a[hi:hi + 128].T if hi + 128 <= S else np.zeros((D, 128), np.float32)
    pos = np.concatenate([np.arange(lo, hi), np.arange(lo - 128, lo), np.arange(hi, hi + 128)])
    cos, sin = rope_tables(np.clip(pos, 0, 8191))
    kq = np.arange(128)
    mL = (kq[:, None] >= kq[None, :]).astype(np.float32)
    mR = (kq[:, None] <= kq[None, :]).astype(np.float32)
    z = np.zeros_like(mL)
    masks = np.stack([mL if c > 0 else z, mL, mR, mR if c < NCORE - 1 else z], axis=1)
    return {"hin": np.ascontiguousarray(hin), "hhalo": np.ascontiguousarray(np.concatenate([hl, hr], axis=1)),
            "cos": cos, "sin": sin, "masks": np.ascontiguousarray(masks),
            "edge": np.tile(np.array([[float(c > 0), float(c < NCORE - 1)]], np.float32), (128, 1))}


def _dump_C(cx, nc, XB, H, yout):
    em = cx.em
    yo_v = yout.h.ap().rearrange("(k p) t -> p k t", p=128)
    with em.scope():
        tmp = [em.sbuf("dbgc%d" % i, [128, LAT], F32) for i in range(2)]
        for k in range(KC):
            t_ = tmp[k % 2]
            src = H.v((SL, k), keys=tuple((k, ti) for ti in range(2))) if H is not None else XB.v((SL, k), keys=tuple((k, ti) for ti in range(2)))
            em.copy("dve", t_.v(), src)
            em.dma("sp", View(yout, yo_v[:, k, :], tuple((k, ti) for ti in range(2))), t_.v(), semof=t_.v())
        em.finish([View(yout, yout.h[:, :], tuple((k, ti) for k in range(KC) for ti in range(2)))], close=False)
    return nc
```

```python
import numpy as np
from contextlib import ExitStack, contextmanager
import concourse.bass as bass
import concourse.mybir as mybir

F32 = mybir.dt.float32
BF16 = mybir.dt.bfloat16
ALU = mybir.AluOpType
AF = mybir.ActivationFunctionType

ENGS = ("pe", "act", "dve", "pool", "sp")


class _St:
    __slots__ = ("w", "r")

    def __init__(self, init):
        self.w = None
        self.r = dict(init)


class Tile:
    def __init__(self, em, h, name, init):
        self.em = em
        self.h = h
        self.name = name
        self.init = init
        self.st = {}
        self.dsem = {}

    def state(self, k):
        s = self.st.get(k)
        if s is None:
            s = self.st[k] = _St(self.init)
        return s

    def v(self, idx=None, keys=(None,)):
        ap = self.h[:] if idx is None else self.h[idx]
        if not isinstance(keys, (tuple, list)):
            keys = (keys,)
        return View(self, ap, tuple(keys))

    def __getitem__(self, idx):
        return View(self, self.h[idx], (None,))


class View:
    __slots__ = ("tile", "ap", "keys")

    def __init__(self, tile, ap, keys):
        self.tile = tile
        self.ap = ap
        self.keys = keys


def _ap(x):
    return x.ap if isinstance(x, View) else x


class Emitter:
    def __init__(self, nc):
        self.nc = nc
        self.stack = ExitStack()
        self.eng = {"pe": nc.tensor, "act": nc.scalar, "dve": nc.vector, "pool": nc.gpsimd, "sp": nc.sync}
        self.sem = {}
        self.count = {e: 0 for e in ENGS}
        self.seen = {e: {} for e in ENGS}
        self.freed = {}
        self.nsem = 0
        self.scopes = []
        for e in ("pe", "act", "dve", "pool"):
            self.sem[e] = self.stack.enter_context(nc.semaphore("s_" + e))
        self.ninstr = 0

    def new_sem(self, name):
        self.nsem += 1
        return self.stack.enter_context(self.nc.semaphore("d%d_%s" % (self.nsem, name)))

    def _mk(self, h, name):
        t = Tile(self, h, name, dict(self.freed))
        if self.scopes:
            self.scopes[-1][1].append(t)
        return t

    def sbuf(self, name, shape, dtype):
        st = self.scopes[-1][0] if self.scopes else self.stack
        self.nsem += 1
        name = "%s_%d" % (name, self.nsem)
        h = st.enter_context(self.nc.sbuf_tensor(name, list(shape), dtype))
        return self._mk(h, name)

    def psum(self, name, shape, dtype):
        st = self.scopes[-1][0] if self.scopes else self.stack
        h = st.enter_context(self.nc.psum_tensor(name, list(shape), dtype))
        return self._mk(h, name)

    def dram(self, name, shape, dtype, kind=None):
        if kind is None:
            h = self.nc.dram_tensor(name, list(shape), dtype)
        else:
            h = self.nc.dram_tensor(name, list(shape), dtype, kind=kind)
        return Tile(self, h, name, {})

    @contextmanager
    def scope(self):
        st = ExitStack()
        tiles = []
        self.scopes.append((st, tiles))
        try:
            yield
        finally:
            self.scopes.pop()
            for t in tiles:
                for s in t.st.values():
                    evs = list(s.r.values())
                    if s.w is not None:
                        evs.append(s.w)
                    for (sem, val) in evs:
                        k = id(sem)
                        if k not in self.freed or self.freed[k][1] < val:
                            self.freed[k] = (sem, val)
            st.close()

    def _deps(self, eng, reads, writes):
        deps = {}

        def add(ev):
            if ev is None:
                return
            sem, val = ev
            k = id(sem)
            if k not in deps or deps[k][1] < val:
                deps[k] = (sem, val)

        for v in reads:
            if not isinstance(v, View):
                continue
            for k in v.keys:
                add(v.tile.state(k).w)
        for v in writes:
            if not isinstance(v, View):
                continue
            for k in v.keys:
                s = v.tile.state(k)
                add(s.w)
                for ev in s.r.values():
                    add(ev)
        waits = []
        seen = self.seen[eng]
        for k, (sem, val) in deps.items():
            if eng == "pe" and sem is self.sem.get("pe"):
                continue
            if seen.get(k, 0) < val:
                seen[k] = val
                waits.append((sem, val))
        return waits

    def _mark(self, ev, reads, writes):
        k0 = id(ev[0])
        for v in reads:
            if not isinstance(v, View):
                continue
            for k in v.keys:
                s = v.tile.state(k)
                if k0 not in s.r or s.r[k0][1] < ev[1]:
                    s.r[k0] = ev
        for v in writes:
            if not isinstance(v, View):
                continue
            for k in v.keys:
                s = v.tile.state(k)
                s.w = ev
                s.r = {}

    def op(self, eng, fn, reads, writes, inc=True):
        waits = self._deps(eng, reads, writes)
        mysem = self.sem[eng]
        ev = (mysem, self.count[eng] + 1)
        if inc:
            self.count[eng] += 1

        e = self.eng[eng]
        for (sem, val) in waits:
            e.wait_ge(sem, val)
        ins = fn(e)
        if inc:
            ins.then_inc(mysem, 1)
        self._mark(ev, reads, writes)
        self.ninstr += 1

    @contextmanager
    def dma_group(self, name):
        g = {"sem": self.new_sem(name), "total": 0, "marks": []}
        self._grp = g
        try:
            yield g
        finally:
            self._grp = None
            ev = (g["sem"], g["total"])
            for (reads, writes) in g["marks"]:
                self._mark(ev, reads, writes)

    def dma(self, eng, out, in_, **kw):
        reads = [in_]
        writes = [out]
        waits = self._deps(eng, reads, writes)
        g = getattr(self, "_grp", None)
        if g is not None:
            kw.pop("semof", None)
            e = self.eng[eng]
            for (s, val) in waits:
                e.wait_ge(s, val)
            e.dma_start(out=_ap(out), in_=_ap(in_), **kw).then_inc(g["sem"], 16)
            g["total"] += 16
            g["marks"].append((reads, writes))
            self.ninstr += 1
            return None
        semof = kw.pop("semof", None)
        tgt = semof if semof is not None else (out if isinstance(out, View) else in_)
        key = (tgt.keys[0])
        t = tgt.tile
        if key not in t.dsem:
            t.dsem[key] = [self.new_sem(t.name), 0]
        rec = t.dsem[key]
        rec[1] += 16
        ev = (rec[0], rec[1])
        o_ap, i_ap = _ap(out), _ap(in_)

        e = self.eng[eng]
        for (s, val) in waits:
            e.wait_ge(s, val)
        e.dma_start(out=o_ap, in_=i_ap, **kw).then_inc(rec[0], 16)
        self._mark(ev, reads, writes)
        self.ninstr += 1
        return ev

    def wait_all(self, eng, views):
        waits = self._deps(eng, [], views)

        e = self.eng[eng]
        for (s, val) in waits:
            e.wait_ge(s, val)

    def mm(self, out, lhsT, rhs, start=True, stop=True, inc=None):
        if inc is None:
            inc = stop
        o, l, r = _ap(out), _ap(lhsT), _ap(rhs)
        self.op("pe", lambda e: e.matmul(o, l, r, start=start, stop=stop), [lhsT, rhs], [out], inc=inc)

    def transpose(self, out, in_, ident, inc=True):
        o, i, d = _ap(out), _ap(in_), _ap(ident)
        self.op("pe", lambda e: e.transpose(o, i, d), [in_, ident], [out], inc=inc)

    def act(self, out, in_, func, bias=None, scale=None, eng="act"):
        o, i = _ap(out), _ap(in_)
        kw = {}
        rd = [in_]
        if bias is not None:
            kw["bias"] = _ap(bias)
            rd.append(bias)
        if scale is not None:
            kw["scale"] = _ap(scale)
            rd.append(scale)
        self.op(eng, lambda e: e.activation(o, i, func, **kw), rd, [out])

    def tt(self, eng, out, in0, in1, op):
        o, a, b = _ap(out), _ap(in0), _ap(in1)
        self.op(eng, lambda e: e.tensor_tensor(o, a, b, op), [in0, in1], [out])

    def ts(self, eng, out, in0, s1, s2=None, op0=ALU.mult, op1=None):
        o, a, x1, x2 = _ap(out), _ap(in0), _ap(s1), _ap(s2)
        rd = [in0, s1, s2]
        if op1 is None:
            self.op(eng, lambda e: e.tensor_scalar(o, a, x1, x2, op0), rd, [out])
        else:
            self.op(eng, lambda e: e.tensor_scalar(o, a, x1, x2, op0, op1), rd, [out])

    def stt(self, eng, out, in0, scalar, in1, op0, op1):
        o, a, s, b = _ap(out), _ap(in0), _ap(scalar), _ap(in1)
        self.op(eng, lambda e: e.scalar_tensor_tensor(o, a, s, b, op0, op1), [in0, scalar, in1], [out])

    def copy(self, eng, out, in_):
        o, i = _ap(out), _ap(in_)
        if eng == "act":
            self.op(eng, lambda e: e.activation(o, i, AF.Identity), [in_], [out])
        else:
            self.op(eng, lambda e: e.tensor_copy(o, i), [in_], [out])

    def memset(self, eng, out, val):
        o = _ap(out)
        self.op(eng, lambda e: e.memset(o, val), [], [out])

    def recip(self, out, in_):
        o, i = _ap(out), _ap(in_)
        self.op("dve", lambda e: e.reciprocal(o, i), [in_], [out])

    def finish(self, final_views=(), close=True):
        if final_views:
            self.wait_all("sp", list(final_views))
        if close:
            self.stack.close()


from concourse.bass_utils import run_bass_kernel_spmd

D = 2048
KC = 16
NCORE = 8
LAT = 1024
CTX = 256
T0 = CTX + LAT
DFF = 5632
FC = 44
EPS = 1e-6
SL = slice(None)


def ckeys(t, k, lo, n):
    g = getattr(t, "grid", None)
    if g is None:
        return tuple((k, ti) for ti, (tl, tn) in enumerate(t.tt) if tl < lo + n and lo < tl + tn)
    return tuple((k, ("g", i)) for i in range(len(g) - 1) if g[i] < lo + n and lo < g[i + 1])


def kt(t, k, ti):
    lo, n = t.tt[ti]
    return t.v((SL, k, slice(lo, lo + n)), keys=ckeys(t, k, lo, n))


def allk(t, k):
    w = t.h[:].shape[-1]
    return t.v((SL, k), keys=ckeys(t, k, 0, w))


class Ring:
    def __init__(self, em, name, shape, dtype, depth):
        self.em = em
        self.t = em.sbuf(name, [128, depth] + list(shape), dtype)
        self.depth = depth
        self.i = 0

    def load(self, src, eng="pool"):
        s = self.i % self.depth
        self.i += 1
        self.em.dma(eng, self.t.v((SL, s), keys=(s,)), src)
        return s

    def view(self, s, idx=()):
        return self.t.v((SL, s) + tuple(idx), keys=(s,))


def stream(ring, srcs, ahead):
    slots = []
    n = len(srcs)
    for i in range(min(ahead, n)):
        slots.append(ring.load(srcs[i]))
    for i in range(n):
        if i + ahead < n:
            slots.append(ring.load(srcs[i + ahead]))
        yield i, slots[i]


class Ctx:
    def __init__(self, nc):
        self.nc = nc
        self.em = em = Emitter(nc)
        self.ps = [em.psum("ps%d" % i, [128, 512], F32) for i in range(8)]
        self.ones = em.sbuf("ones_bf", [128, 128], BF16)
        em.memset("dve", self.ones.v(), 1.0)
        self.epsc = em.sbuf("epsc", [128, 1], F32)
        em.memset("dve", self.epsc.v(), EPS)


def rmsnorm_mod(cx, H, xn, ranges, scratch_ps, out_fn=None):
    em = cx.em
    with em.scope():
        rstd = em.sbuf("rstd", [128, 512], F32)
        ntmp = [em.sbuf("ntmp%d" % i, [128, 512], F32) for i in range(2)]
        for (hti, xti, Af, Bf) in ranges:
            n = H.tt[hti][1]
            ps = cx.ps[scratch_ps]
            for k in range(KC):
                if k % 2 == 0:
                    em.act(kt(xn, k, xti), kt(H, k, hti), AF.Square)
                else:
                    em.tt("pool", kt(xn, k, xti), kt(H, k, hti), kt(H, k, hti), ALU.mult)
            for k in range(KC):
                em.mm(ps.v((SL, slice(0, n))), cx.ones.v(), kt(xn, k, xti), start=(k == 0), stop=(k == KC - 1))
            em.act(rstd.v((SL, slice(0, n))), ps.v((SL, slice(0, n))), AF.Sqrt, bias=cx.epsc.v(), scale=1.0 / D)
            em.recip(rstd.v((SL, slice(0, n))), rstd.v((SL, slice(0, n))))
            for k in range(KC):
                tmp = ntmp[k % 2]
                em.tt("dve", tmp.v((SL, slice(0, n))), kt(H, k, hti), rstd.v((SL, slice(0, n))), ALU.mult)
                if out_fn is not None:
                    out_fn(k, xti, tmp.v((SL, slice(0, n))), n)
                elif isinstance(Af, list):
                    xlo = xn.tt[xti][0]
                    for (off, ns, Afs, Bfs) in Af:
                        em.act(xcols(xn, k, xlo + off, ns), tmp.v((SL, slice(off, off + ns))), AF.Identity, bias=Bfs(k), scale=Afs(k))
                else:
                    em.act(kt(xn, k, xti), tmp.v((SL, slice(0, n))), AF.Identity, bias=Bf(k), scale=Af(k))


def ffn(cx, H, xn, tiles, wi_d, wo_d, gh, G=4):
    em = cx.em
    with em.scope():
        wi_r = Ring(em, "wi_r", [KC, 2, 128], BF16, 3)
        wo_r = Ring(em, "wo_r", [D], BF16, 2 * G)
        ntt = len(tiles)
        actb = em.sbuf("actb", [128, 2 * G, max(lo_ + n_ for lo_, n_ in xn.tt)], BF16)
        sg = [em.sbuf("sg%d" % i, [128, 512], F32) for i in range(2)]
        wo_slots = {}
        wo_next = 0
        cnt = 0
        for f, ws in stream(wi_r, [wi_d[f] for f in range(FC)], 2):
            while wo_next < min(FC, f + G + 1):
                wo_slots[wo_next] = wo_r.load(wo_d[wo_next])
                wo_next += 1
            aslot = f % (2 * G)
            for (hti, xti, vec) in tiles:
                lo, n = xn.tt[xti]
                b = (cnt % 2) * 2
                cnt += 1
                gp, up = cx.ps[b], cx.ps[b + 1]
                for k in range(KC):
                    em.mm(gp.v((SL, slice(0, n))), wi_r.view(ws, (k, 0)), kt(xn, k, xti), start=(k == 0), stop=(k == KC - 1))
                for k in range(KC):
                    em.mm(up.v((SL, slice(0, n))), wi_r.view(ws, (k, 1)), kt(xn, k, xti), start=(k == 0), stop=(k == KC - 1))
                s = sg[cnt % 2]
                em.act(s.v((SL, slice(0, n))), gp.v((SL, slice(0, n))), AF.Silu)
                em.tt("dve", actb.v((SL, aslot, slice(lo, lo + n)), keys=((aslot, xti),)), s.v((SL, slice(0, n))),
                      up.v((SL, slice(0, n))), ALU.mult)
            if f % G == G - 1:
                f0 = f - G + 1
                oc = 0
                for d in range(KC):
                    for (hti, xti, vec) in tiles:
                        lo, n = xn.tt[xti]
                        op = cx.ps[4 + (oc % 2)]
                        oc += 1
                        for j in range(G):
                            ff = f0 + j
                            em.mm(op.v((SL, slice(0, n))), wo_r.view(wo_slots[ff], (slice(d * 128, (d + 1) * 128),)),
                                  actb.v((SL, ff % (2 * G), slice(lo, lo + n)), keys=((ff % (2 * G), xti),)),
                                  start=(j == 0), stop=(j == G - 1))
                        if isinstance(vec, list):
                            hlo = H.tt[hti][0]
                            for (off, ns, vv) in vec:
                                hv = xcols(H, d, hlo + off, ns)
                                em.stt("dve", hv, op.v((SL, slice(off, off + ns))), gh(vv, d), hv, ALU.mult, ALU.add)
                        else:
                            em.stt("dve", kt(H, d, hti), op.v((SL, slice(0, n))), gh(vec, d), kt(H, d, hti), ALU.mult, ALU.add)


def norm_scratch(cx):
    pass


def mk_ranges(tiles, tvec, Aof, Bof):
    out = []
    for ti, tv in zip(tiles, tvec):
        if isinstance(tv, list):
            out.append((ti, ti, [(off, ns, Aof(v), Bof(v)) for (off, ns, v) in tv], None))
        else:
            out.append((ti, ti, Aof(tv), Bof(tv)))
    return out


def prep_mod(cx, mod_d, nw_d, nsub):
    em = cx.em
    m = em.sbuf("modt", [128, 2, nsub * 3, KC], F32)
    nw = em.sbuf("nwt", [128, nsub, KC], F32)
    A = em.sbuf("modA", [128, 2, nsub, KC], F32)
    em.dma("sp", m.v(), mod_d)
    em.dma("sp", nw.v(), nw_d)
    for v in range(2):
        for s in range(nsub):
            em.ts("dve", A.v((SL, v, s)), m.v((SL, v, 3 * s + 1)), 1.0, None, ALU.add)
            em.tt("dve", A.v((SL, v, s)), A.v((SL, v, s)), nw.v((SL, s)), ALU.mult)
    return m, A


TT1280 = [(0, 256), (256, 512), (768, 512)]
CO = CTX // NCORE
TA = CO + LAT
TW = TA // 3
TTA = [(0, TW), (TW, TW), (2 * TW, TW)]
GRID_A = [0, CO, TW, 2 * TW, TA]
CB = CTX - CO
TTF = [(CB, TW), (CB + TW, TW), (CB + 2 * TW, TW)]
GRID_B = sorted(set([0, CB, 256, 768, T0] + [CB + i * TW for i in range(4)]))
SEG0 = [(0, CO, 1), (CO, TW - CO, 0)]


def build_A():
    nc = bass.Bass("TRN2", target_bir_lowering=False)
    cx = Ctx(nc)
    em = cx.em
    xT = nc.dram_tensor("xT", [D, TA], F32, kind="ExternalInput")
    mod = nc.dram_tensor("mod", [128, 2, 3, KC], F32, kind="ExternalInput")
    nw = nc.dram_tensor("nw", [128, 1, KC], F32, kind="ExternalInput")
    wi = nc.dram_tensor("wi", [FC, 128, KC, 2, 128], F32, kind="ExternalInput")
    wo = nc.dram_tensor("wo", [FC, 128, D], F32, kind="ExternalInput")
    hout = em.dram("hout", [D, TA], F32, kind="ExternalOutput")
    norm_scratch(cx)
    H = em.sbuf("H", [128, KC, TA], F32)
    H.tt = TTA
    H.grid = GRID_A
    xn = em.sbuf("xn", [128, KC, TA], BF16)
    xn.tt = TTA
    xn.grid = GRID_A
    xT_v = xT.ap().rearrange("(k p) t -> p k t", p=128)
    with em.dma_group("Hld"):
        for k in range(KC):
            em.dma("sp", allk(H, k), xT_v[:, k, :])
    m, A = prep_mod(cx, mod[:, :, :, :], nw[:, :, :], 1)
    vec_of = [SEG0, 0, 0]
    rmsnorm_mod(cx, H, xn, mk_ranges(range(3), vec_of, lambda v: (lambda k: A.v((SL, v, 0, slice(k, k + 1)))),
                                     lambda v: (lambda k: m.v((SL, v, 0, slice(k, k + 1))))), 6)
    gh = em.sbuf("gh", [128, 2, KC], F32)
    em.ts("dve", gh.v(), m.v((SL, SL, 2)), 0.5, None, ALU.mult)
    ffn(cx, H, xn, [(ti, ti, vec_of[ti]) for ti in range(3)], wi, wo,
        lambda vec, d: gh.v((SL, vec, slice(d, d + 1))))
    ho_v = hout.h.ap().rearrange("(k p) t -> p k t", p=128)
    with em.dma_group("hst"):
        for k in range(KC):
            em.dma("sp", View(hout, ho_v[:, k, :], ((k,),)), allk(H, k))
    em.finish([View(hout, hout.h[:, :], tuple((k,) for k in range(KC)))])
    return nc


def lay_wi(w):
    return np.ascontiguousarray(w.reshape(KC, 128, 2, FC, 128).transpose(3, 1, 0, 2, 4))


def lay_wo(w):
    return np.ascontiguousarray(w.reshape(FC, 128, D))


def lay_vec(v):
    v = np.asarray(v)
    lead = v.shape[:-1]
    r = v.reshape(lead + (KC, 128))
    return np.ascontiguousarray(np.moveaxis(r, -1, 0))


def xcols(xn, k, lo, n):
    return xn.v((SL, k, slice(lo, lo + n)), keys=ckeys(xn, k, lo, n))


def proj_fm(cx, xn, xtis, wsrcs, ring, evac, banks=(0, 1, 2, 3)):
    em = cx.em
    cnt = 0
    for ci, ws in stream(ring, wsrcs, ring.depth - 1):
        for xti in xtis:
            lo, n = xn.tt[xti]
            ps = cx.ps[banks[cnt % len(banks)]]
            cnt += 1
            for k in range(KC):
                em.mm(ps.v((SL, slice(0, n))), ring.view(ws, (k,)), kt(xn, k, xti), start=(k == 0), stop=(k == KC - 1))
            evac(ci, xti, ps, n)


def proj_tm(cx, xn, blocks, wsrcs, wt, evac, banks=(0, 1, 2, 3)):
    em = cx.em
    nch = len(wsrcs)
    for c, src in enumerate(wsrcs):
        em.dma("pool", wt.v((SL, c), keys=(c,)), src)
    wkeys = tuple(range(nch))
    for bi, lo in enumerate(blocks):
        ps = cx.ps[banks[bi % len(banks)]]
        for k in range(KC):
            em.mm(ps.v((SL, slice(0, nch * 128))), xcols(xn, k, lo, 128), wt.v((SL, SL, k, SL), keys=wkeys),
                  start=(k == 0), stop=(k == KC - 1))
        evac(bi, ps)


ROPE_ADD_ENG = "dve"


def rope_evac(cx, ps, n, out, cosv, sinv):
    em = cx.em
    i = cx.rcnt % 2
    cx.rcnt += 1
    xb = cx.rxb[i]
    em.copy("act", xb.v((SL, slice(0, n))), ps.v((SL, slice(0, n))))
    sw = cx.ps[6 + i]
    em.mm(sw.v((SL, slice(0, n))), cx.Pm.v(), xb.v((SL, slice(0, n))))
    t1, t2 = cx.rt1[i], cx.rt2[i]
    em.tt("dve", t1.v((SL, slice(0, n))), xb.v((SL, slice(0, n))), cosv, ALU.mult)
    em.tt("dve", t2.v((SL, slice(0, n))), sw.v((SL, slice(0, n))), sinv, ALU.mult)
    em.tt(ROPE_ADD_ENG, out, t1.v((SL, slice(0, n))), t2.v((SL, slice(0, n))), ALU.add)


def rope_setup(cx, cos_d, sin_d, pm_d, ncols):
    em = cx.em
    cx.rcnt = 0
    cx.COS = em.sbuf("COS", [128, ncols], F32)
    cx.SIN = em.sbuf("SIN", [128, ncols], F32)
    cx.Pm = em.sbuf("Pm", [128, 128], BF16)
    em.dma("sp", cx.COS.v(), cos_d)
    em.dma("sp", cx.SIN.v(), sin_d)
    em.dma("pool", cx.Pm.v(), pm_d)
    cx.rxb = [em.sbuf("rxb%d" % i, [128, 512], BF16) for i in range(2)]
    cx.rt1 = [em.sbuf("rt1%d" % i, [128, 512], F32) for i in range(2)]
    cx.rt2 = [em.sbuf("rt2%d" % i, [128, 512], F32) for i in range(2)]


def attn_tile(cx, q4, qh, slots, scale, esink4, out4, shared):
    em = cx.em
    ns = len(slots)
    pts = []
    for si, sl in enumerate(slots):
        sp = cx.ps[si % 2]
        if shared:
            em.mm(sp.v(), sl["k"], q4)
        else:
            for hh in range(4):
                em.mm(sp.v((SL, slice(hh * 128, (hh + 1) * 128))), sl["k"](hh), qh(hh), inc=(hh == 3))
        pt = cx.ptr.t.v((SL, cx.ptr.i % cx.ptr.depth), keys=(cx.ptr.i % cx.ptr.depth,))
        cx.ptr.i += 1
        if sl.get("bias") is not None:
            tmp = cx.atmp[si % 2]
            em.stt("dve", tmp.v(), sp.v(), scale, sl["bias"], ALU.mult, ALU.add)
            em.act(pt, tmp.v(), AF.Exp)
        elif sl.get("mask") is not None:
            tmp = cx.atmp[si % 2]
            em.act(tmp.v(), sp.v(), AF.Exp, scale=scale)
            em.tt("dve", pt, tmp.v(), sl["mask"], ALU.mult)
        else:
            em.act(pt, sp.v(), AF.Exp, scale=scale)
        pts.append(pt)
    a = cx.acnt % 2
    cx.acnt += 1
    den, o = cx.ps[2 + a], cx.ps[4 + a]
    for si in range(ns):
        em.mm(den.v(), cx.ones.v(), pts[si], start=(si == 0), stop=(si == ns - 1))
    if shared:
        for si, sl in enumerate(slots):
            em.mm(o.v(), sl["v"], pts[si], start=(si == 0), stop=(si == ns - 1))
    else:
        for hh in range(4):
            for si, sl in enumerate(slots):
                cs = slice(hh * 128, (hh + 1) * 128)
                em.mm(o.v((SL, cs)), sl["v"](hh), View(pts[si].tile, pts[si].ap[:, cs], pts[si].keys),
                      start=(si == 0), stop=(si == ns - 1), inc=(hh == 3 and si == ns - 1))
    rd = cx.rden[a]

    def v3(t):
        return View(t, t.h[:].rearrange("p (h q) -> p h q", h=4), (None,))

    if esink4 is not None:
        em.tt("dve", v3(rd), v3(den), esink4, ALU.add)
        em.recip(rd.v(), rd.v())
    else:
        em.recip(rd.v(), den.v())
    em.tt("dve", out4, v3(o), v3(rd), ALU.mult)


def attn_setup(cx):
    em = cx.em
    cx.acnt = 0
    cx.ptr = Ring(em, "ptr", [512], BF16, 8)
    cx.atmp = [em.sbuf("atmp%d" % i, [128, 512], F32) for i in range(2)]
    cx.rden = [em.sbuf("rden%d" % i, [128, 512], F32) for i in range(2)]


TTB = [(0, 256), (256, 512), (768, 512), (1280, 256)]
ATT_SCALE = 128.0 ** -0.5


def build_B(upto=4):
    nc = bass.Bass("TRN2", target_bir_lowering=False)
    cx = Ctx(nc)
    em = cx.em
    din = lambda name, shape: nc.dram_tensor(name, list(shape), F32, kind="ExternalInput")
    hin = din("hin", [D, T0])
    hhalo = din("hhalo", [D, 256])
    mod0 = din("mod0", [128, 2, 6, KC])
    nw0 = din("nw0", [128, 2, KC])
    win = din("win", [36, 128, KC, 128])
    cosd, sind = din("cos", [128, LAT + 256]), din("sin", [128, LAT + 256])
    pmd = din("pm", [128, 128])
    masks = din("masks", [128, 4, 128])
    sinkd = din("sink", [8])
    convd = din("convp", [128, 8, 4])
    edged = din("edge", [128, 2])
    hout = em.dram("hout", [D, T0], F32, kind="ExternalOutput")
    lfb = em.dram("lfb", [2, 4, 128, 256], F32, kind="ExternalOutput") if upto >= 4 else None
    norm_scratch(cx)
    XB = em.sbuf("XB", [128, KC, T0], BF16)
    XB.tt = TT1280
    XB.grid = GRID_B
    m0, A0 = prep_mod(cx, mod0[:, :, :, :], nw0[:, :, :], 2)
    vec_of = [1, 0, 0, 0]
    hin_v = hin.ap().rearrange("(k p) t -> p k t", p=128)
    hh_v = hhalo.ap().rearrange("(k p) t -> p k t", p=128)

    if upto < 10:
        with em.scope():
            xm = em.sbuf("xm", [128, KC, 1536], BF16)
            xm.tt = TTB
            with em.scope():
                H1 = em.sbuf("H1", [128, KC, 1536], F32)
                H1.tt = TTB
                with em.dma_group("H1ld"):
                    for k in range(KC):
                        em.dma("sp", H1.v((SL, k, slice(0, T0)), keys=tuple((k, ti) for ti in range(3))), hin_v[:, k, :])
                        em.dma("sp", kt(H1, k, 3), hh_v[:, k, :])
                rmsnorm_mod(cx, H1, xm, [(ti, ti, (lambda k, v=vec_of[ti]: A0.v((SL, v, 0, slice(k, k + 1)))),
                                          (lambda k, v=vec_of[ti]: m0.v((SL, v, 0, slice(k, k + 1))))) for ti in range(4)], 6)
            with em.scope():
                rope_setup(cx, cosd[:, :], sind[:, :], pmd[:, :], LAT + 256)
                attn_setup(cx)
                ring = Ring(em, "wring", [KC, 128], BF16, 4)
                QT = em.sbuf("QT", [128, 8, T0], BF16)
                KT = em.sbuf("KT", [128, 2, 1536], BF16)
                V = em.sbuf("V", [128, 12, 256], BF16)
                mk = em.sbuf("mk", [128, 4, 128], F32)
                em.dma("sp", mk.v(), masks[:, :, :])
                es = em.sbuf("es", [128, 8], F32)
                em.dma("sp", es.v(), sinkd.ap().partition_broadcast(128))
                em.act(es.v(), es.v(), AF.Exp)
                cvp = em.sbuf("cvp", [128, 8, 4], F32)
                em.dma("sp", cvp.v(), convd[:, :, :])
                edg = em.sbuf("edg", [128, 2], F32)
                em.dma("sp", edg.v(), edged[:, :])

                def cs_views(xti):
                    lo, n = TTB[xti]
                    return cx.COS.v((SL, slice(lo - 256, lo - 256 + n))), cx.SIN.v((SL, slice(lo - 256, lo - 256 + n)))

                if upto <= -4:
                    return _finish_B(cx, nc, XB, None, hout, lfb, stage=0)
                def evac_k(ci, xti, ps, n):
                    lo, _ = TTB[xti]
                    out = KT.v((SL, ci, slice(lo, lo + n)), keys=((ci, xti),))
                    if xti == 0:
                        em.copy("act", out, ps.v((SL, slice(0, n))))
                    else:
                        c, s_ = cs_views(xti)
                        rope_evac(cx, ps, n, out, c, s_)
                proj_fm(cx, xm, [0, 1, 2, 3], [win[8 + c] for c in range(2)], ring, evac_k)

                if upto <= -3:
                    return _finish_B(cx, nc, XB, None, hout, lfb, stage=0)
                with em.scope():
                    wv = em.sbuf("wv", [128, 2, KC, 128], BF16)

                    def evac_v(bi, ps):
                        em.copy("act", V.v((SL, bi), keys=(bi,)), ps.v((SL, slice(0, 256))))
                    proj_tm(cx, xm, [128 * b for b in range(12)], [win[10 + c] for c in range(2)], wv, evac_v)

                def evac_q(ci, xti, ps, n):
                    lo, _ = TTB[xti]
                    out = QT.v((SL, ci, slice(lo, lo + n)), keys=((ci, xti),))
                    if xti == 0:
                        em.copy("act", out, ps.v((SL, slice(0, n))))
                    else:
                        c, s_ = cs_views(xti)
                        rope_evac(cx, ps, n, out, c, s_)
                proj_fm(cx, xm, [0, 1, 2], [win[c] for c in range(8)], ring, evac_q)

                if upto <= -2:
                    return _finish_B(cx, nc, XB, None, hout, lfb, stage=0)
                with em.scope():
                    csb = em.sbuf("csb", [128, T0], F32)
                    z = em.sbuf("z", [128, T0 + 4], F32)
                    acc = em.sbuf("cacc", [128, T0], F32)
                    c2 = em.sbuf("c2", [128, 2], F32)
                    em.memset("dve", z.v(None, keys=(0, 1, 2, "h")), 0.0)
                    zoff = [1, 3, 3]

                    for j in range(8):
                        st = {}

                        def evac_c(ci, xti, ps, n, st=st):
                            lo, _ = TTB[xti]
                            if ci == 0:
                                em.copy("act", csb.v((SL, slice(lo, lo + n)), keys=(xti,)), ps.v((SL, slice(0, n))))
                            elif ci == 1:
                                em.tt("dve", z.v((SL, slice(lo + zoff[xti], lo + zoff[xti] + n)), keys=(xti,)),
                                      csb.v((SL, slice(lo, lo + n)), keys=(xti,)), ps.v((SL, slice(0, n))), ALU.mult)
                                if xti == 0:
                                    em.tt("dve", z.v((SL, slice(CB, CB + 1)), keys=(0,)), z.v((SL, slice(CB, CB + 1)), keys=(0,)),
                                          edg.v((SL, slice(0, 1))), ALU.mult)
                                    em.tt("dve", z.v((SL, slice(257, 258)), keys=(0,)), z.v((SL, slice(1, 2)), keys=(0,)),
                                          edg.v((SL, slice(1, 2))), ALU.mult)
                            else:
                                if xti == 0:
                                    for (zl, al, n2, zkeys, akey) in ((1, 0, 256, (0,), 0), (259, 256, 512, (1, 2, "h"), 1), (771, 768, 512, (1, 2, "h"), 2)):
                                        w = lambda q: cvp.v((SL, j, slice(q, q + 1)))
                                        a_ = acc.v((SL, slice(al, al + n2)), keys=(akey,))
                                        em.ts("dve", a_, z.v((SL, slice(zl - 1, zl - 1 + n2)), keys=zkeys), w(0), None, ALU.mult)
                                        em.stt("dve", a_, z.v((SL, slice(zl, zl + n2)), keys=zkeys), w(1), a_, ALU.mult, ALU.add)
                                        em.stt("dve", a_, z.v((SL, slice(zl + 1, zl + 1 + n2)), keys=zkeys), w(2), a_, ALU.mult, ALU.add)
                                em.stt("dve", kt(XB, 8 + j, xti), acc.v((SL, slice(lo, lo + n)), keys=(xti,)),
                                       cvp.v((SL, j, slice(3, 4))), ps.v((SL, slice(0, n))), ALU.add, ALU.mult)

                        slots = []
                        srcs = [win[20 + j], win[28 + j], win[12 + j]]
                        cnt = 0
                        for ci, ws in stream(ring, srcs, ring.depth - 1):
                            for xti in [0, 1, 2]:
                                lo, n = TTB[xti]
                                ps = cx.ps[cnt % 4]
                                cnt += 1
                                for k in range(KC):
                                    em.mm(ps.v((SL, slice(0, n))), ring.view(ws, (k,)), kt(xm, k, xti), start=(k == 0), stop=(k == KC - 1))
                                evac_c(ci, xti, ps, n)
                            if ci < 2:
                                ps = cx.ps[cnt % 4]
                                cnt += 1
                                for k in range(KC):
                                    em.mm(ps.v((SL, slice(0, 2))), ring.view(ws, (k,)), xm.v((SL, k, slice(1407, 1409)), keys=((k, 3),)),
                                          start=(k == 0), stop=(k == KC - 1))
                                if ci == 0:
                                    em.copy("act", c2.v(), ps.v((SL, slice(0, 2))))
                                else:
                                    em.tt("dve", c2.v(), c2.v(), ps.v((SL, slice(0, 2))), ALU.mult)
                                    em.tt("dve", c2.v(), c2.v(), edg.v(), ALU.mult)
                                    em.copy("dve", z.v((SL, slice(258, 259)), keys=("h",)), c2.v((SL, slice(0, 1))))
                                    em.copy("dve", z.v((SL, slice(1283, 1284)), keys=("h",)), c2.v((SL, slice(1, 2))))

                if upto <= -1:
                    return _finish_B(cx, nc, XB, None, hout, lfb, stage=0)
                def es4(g):
                    return View(es, es.h[:, 4 * g:4 * g + 4].unsqueeze(2).broadcast_to([128, 4, 128]), (None,))

                def mk4(i):
                    return View(mk, mk.h[:, i, :].unsqueeze(1).broadcast_to([128, 4, 128]), (None,))

                def kslot(g, blk):
                    if blk == "c0" or blk == "c1":
                        b = 0 if blk == "c0" else 1
                        return KT.v((SL, g, slice(128 * b, 128 * b + 128)), keys=((g, 0),)), V.v((SL, b, slice(128 * g, 128 * g + 128)), keys=(b,))
                    if blk == -1:
                        col, vb, key = 1280, 10, (g, 3)
                    elif blk == 8:
                        col, vb, key = 1408, 11, (g, 3)
                    else:
                        col, vb = 256 + 128 * blk, 2 + blk
                        key = (g, 1 if blk < 4 else 2)
                    return KT.v((SL, g, slice(col, col + 128)), keys=(key,)), V.v((SL, vb, slice(128 * g, 128 * g + 128)), keys=(vb,))

                em.memset("dve", XB.v((SL, slice(0, 8), slice(0, 128)), keys=tuple(kk for c_ in range(8) for kk in ckeys(XB, c_, 0, 128))), 0.0)
                for g in range(2):
                    for qb in range(10):
                        if qb == 0:
                            continue
                        col = 128 * qb
                        xti = 0 if qb < 2 else (1 if qb < 6 else 2)
                        q4 = QT.v((SL, slice(4 * g, 4 * g + 4), slice(col, col + 128)), keys=tuple((4 * g + hh, xti) for hh in range(4)))
                        out4 = XB.v((SL, slice(4 * g, 4 * g + 4), slice(col, col + 128)),
                                    keys=tuple(kk for hh in range(4) for kk in ckeys(XB, 4 * g + hh, col, 128)))
                        slots = []
                        if qb >= 2:
                            n_ = qb - 2
                            kL, vL = kslot(g, n_ - 1)
                            kM, vM = kslot(g, n_)
                            kR, vR = kslot(g, n_ + 1)
                            slots.append(dict(k=kL, v=vL, mask=mk4(0 if n_ == 0 else 1)))
                            slots.append(dict(k=kM, v=vM))
                            slots.append(dict(k=kR, v=vR, mask=mk4(3 if n_ == 7 else 2)))
                        for cb in ("c0", "c1"):
                            kc_, vc_ = kslot(g, cb)
                            slots.append(dict(k=kc_, v=vc_))
                        attn_tile(cx, q4, None, slots, ATT_SCALE, es4(g), out4, True)
    if upto < 1:
        return _finish_B(cx, nc, XB, None, hout, lfb, stage=0)

    wout = din("wout", [16, 128, KC, 128])
    H = em.sbuf("H", [128, KC, T0], F32)
    H.tt = TT1280
    H.grid = GRID_B
    with em.dma_group("Hld"):
        for k in range(KC):
            em.dma("sp", allk(H, k), hin_v[:, k, :])
    vec3 = [1, 0, 0]
    with em.scope():
        ring = Ring(em, "wring2", [KC, 128], BF16, 4)

        def evac_o(ci, xti, ps, n):
            em.stt("dve", kt(H, ci, xti), ps.v((SL, slice(0, n))), m0.v((SL, vec3[xti], 2, slice(ci, ci + 1))), kt(H, ci, xti), ALU.mult, ALU.add)
        if upto < 10:
            proj_fm(cx, XB, [0, 1, 2], [wout[d] for d in range(16)], ring, evac_o)
    if upto < 2:
        return _finish_B(cx, nc, XB, H, hout, lfb, stage=1)

    H.tt = TTF
    XB.tt = TTF
    wib, wob = din("wib", [FC, 128, KC, 2, 128]), din("wob", [FC, 128, D])
    gh = em.sbuf("gh", [128, 2, KC], F32)
    em.ts("dve", gh.v(), m0.v((SL, SL, 5)), 0.5, None, ALU.mult)
    vecF = [SEG0, 0, 0]
    rmsnorm_mod(cx, H, XB, mk_ranges(range(3), vecF, lambda v: (lambda k: A0.v((SL, v, 1, slice(k, k + 1)))),
                                     lambda v: (lambda k: m0.v((SL, v, 3, slice(k, k + 1))))), 6)
    if upto < 10:
        ffn(cx, H, XB, [(ti, ti, vecF[ti]) for ti in range(3)], wib, wob, lambda vec, d: gh.v((SL, vec, slice(d, d + 1))))
    if upto < 3:
        return _finish_B(cx, nc, XB, H, hout, lfb, stage=2)

    wia, woa = din("wia", [FC, 128, KC, 2, 128]), din("woa", [FC, 128, D])
    mod1 = din("mod1", [128, 2, 6, KC])
    nw1 = din("nw1", [128, 2, KC])
    m1, A1 = prep_mod2(cx, mod1[:, :, :, :], nw1[:, :, :], 2)
    gh1 = em.sbuf("gh1", [128, 2, KC], F32)
    em.ts("dve", gh1.v(), m1.v((SL, SL, 2)), 0.5, None, ALU.mult)
    rmsnorm_mod(cx, H, XB, mk_ranges(range(3), vecF, lambda v: (lambda k: A1.v((SL, v, 0, slice(k, k + 1)))),
                                     lambda v: (lambda k: m1.v((SL, v, 0, slice(k, k + 1))))), 6)
    if upto < 10:
        ffn(cx, H, XB, [(ti, ti, vecF[ti]) for ti in range(3)], wia, woa, lambda vec, d: gh1.v((SL, vec, slice(d, d + 1))))
    ho_v = hout.h.ap().rearrange("(k p) t -> p k t", p=128)
    with em.dma_group("hst"):
        for k in range(KC):
            em.dma("sp", View(hout, ho_v[:, k, :], ((k,),)), allk(H, k))
    if upto < 4:
        return _finish_B(cx, nc, XB, None, hout, lfb, stage=3)

    win1 = din("win1", [12, 128, KC, 128])
    decd = din("dec", [2, 4])
    expd = din("expo", [128, 2, 8])
    H.tt = TT1280
    XB.tt = TT1280
    rmsnorm_mod(cx, H, XB, [(ti, ti, (lambda k: A1.v((SL, 0, 1, slice(k, k + 1)))),
                             (lambda k: m1.v((SL, 0, 3, slice(k, k + 1))))) for ti in (1, 2)], 6)
    with em.scope():
        rope_setup(cx, cosd[:, 0:LAT], sind[:, 0:LAT], pmd[:, :], LAT)
        ring = Ring(em, "wring3", [KC, 128], BF16, 4)
        RK = em.sbuf("RK", [128, 4, LAT], BF16)
        RV = em.sbuf("RV", [128, 8, 1024], BF16)
        ret_local_sums(cx, XB, win1, ring, RK, RV, decd, expd, lfb)
    return _finish_B(cx, nc, XB, None, hout, lfb, stage=4)


def prep_mod2(cx, mod_d, nw_d, nsub):
    em = cx.em
    m = em.sbuf("modt2", [128, 2, nsub * 3, KC], F32)
    nw = em.sbuf("nwt2", [128, nsub, KC], F32)
    A = em.sbuf("modA2", [128, 2, nsub, KC], F32)
    em.dma("sp", m.v(), mod_d)
    em.dma("sp", nw.v(), nw_d)
    for v in range(2):
        for s in range(nsub):
            em.ts("dve", A.v((SL, v, s)), m.v((SL, v, 3 * s + 1)), 1.0, None, ALU.add)
            em.tt("dve", A.v((SL, v, s)), A.v((SL, v, s)), nw.v((SL, s)), ALU.mult)
    return m, A


def _finish_B(cx, nc, XB, H, hout, lfb, stage):
    em = cx.em
    finals = []
    if stage < 3:
        ho_v = hout.h.ap().rearrange("(k p) t -> p k t", p=128)
        if H is None:
            with em.scope():
                tmp = em.sbuf("dbg", [128, T0], F32)
                for k in range(KC):
                    em.copy("dve", tmp.v(), allk(XB, k))
                    em.dma("sp", View(hout, ho_v[:, k, :], ((k,),)), tmp.v(), semof=tmp.v())
        else:
            with em.dma_group("hst"):
                for k in range(KC):
                    em.dma("sp", View(hout, ho_v[:, k, :], ((k,),)), allk(H, k))
    finals.append(View(hout, hout.h[:, :], tuple((k,) for k in range(KC))))
    if stage >= 4 and lfb is not None:
        finals.append(View(lfb, lfb.h[:, :, :, :], tuple((d, h) for d in range(2) for h in range(4))))
    em.finish(finals, close=(len(em.scopes) == 0))
    return nc


RET_SCALE = 128.0 ** -0.5


def psbf(ps, n=1024):
    return View(ps, ps.h.bitcast(BF16)[:, 0:n], (None,))


def make_ident(cx):
    em = cx.em
    idf = em.sbuf("identf", [128, 128], F32)
    cx.ident = em.sbuf("ident", [128, 128], BF16)
    em.memset("dve", idf.v(), 1.0)
    em.op("pool", lambda e: e.affine_select(idf.h[:], idf.h[:], [[-1, 128]], ALU.is_equal, 0.0, base=0, channel_multiplier=1),
          [idf.v()], [idf.v()])
    em.copy("dve", cx.ident.v(), idf.v())


def load_lg(cx, decd):
    em = cx.em
    lg = em.sbuf("lg", [128, 8], F32)
    em.dma("sp", lg.v(), decd.ap().rearrange("a b -> (a b)").partition_broadcast(128))
    em.act(lg.v(), lg.v(), AF.Sigmoid)
    em.act(lg.v(), lg.v(), AF.Ln)
    lns = em.sbuf("lns", [128, 1], F32)
    em.memset("dve", lns.v(), float(np.log(RET_SCALE)))
    return lg, lns


def ret_local_sums(cx, XB, win1, ring, RK, RV, decd, expd, lfb):
    em = cx.em
    make_ident(cx)
    lg, lns = load_lg(cx, decd)
    ex = em.sbuf("ex", [128, 2, 8], F32)
    em.dma("sp", ex.v(), expd[:, :, :])
    wts = em.sbuf("wts", [128, 2, 4, 8], F32)
    for d_ in range(2):
        for h in range(4):
            em.act(wts.v((SL, d_, h)), ex.v((SL, d_)), AF.Exp, bias=lns.v(), scale=lg.v((SL, slice(4 * d_ + h, 4 * d_ + h + 1))))

    def evac_k(ci, xti, ps, n):
        lo, _ = XB.tt[xti]
        rope_evac(cx, ps, n, RK.v((SL, ci, slice(lo - 256, lo - 256 + n)), keys=((ci, xti),)),
                  cx.COS.v((SL, slice(lo - 256, lo - 256 + n))), cx.SIN.v((SL, slice(lo - 256, lo - 256 + n))))
    proj_fm(cx, XB, [1, 2], [win1[c] for c in range(4)], ring, evac_k)
    with em.scope():
        wt = em.sbuf("wt4", [128, 4, KC, 128], BF16)
        for half in range(2):
            def evac_v(bi, ps, half=half):
                em.copy("act", RV.v((SL, bi, slice(512 * half, 512 * half + 512)), keys=((bi, half),)), ps.v())
            proj_tm(cx, XB, [256 + 128 * b for b in range(8)], [win1[4 + 4 * half + c] for c in range(4)], wt, evac_v)
    ktok = em.sbuf("ktok", [128, 4, 8, 128], BF16)
    bi_ = 0
    for h in range(4):
        for n4 in range(2):
            pb = cx.ps[6 + (bi_ % 2)]
            bi_ += 1
            for j in range(4):
                n = 4 * n4 + j
                xti = 1 if n < 4 else 2
                em.transpose(View(pb, pb.h.bitcast(BF16)[:, 128 * j:128 * j + 128], (None,)),
                             RK.v((SL, h, slice(128 * n, 128 * n + 128)), keys=((h, xti),)), cx.ident.v(), inc=(j == 3))
            em.copy("act", View(ktok, ktok.h[:, h, 4 * n4:4 * n4 + 4, :].rearrange("p a b -> p (a b)"), tuple((h, 4 * n4 + j) for j in range(4))),
                    psbf(pb, 512))
    vz = [em.sbuf("vz%d" % i, [128, 256], BF16) for i in range(16)]
    lsb = [em.sbuf("lsb%d" % i, [128, 256], F32) for i in range(2)]
    cnt = 0
    for d_ in range(2):
        for h in range(4):
            ps = cx.ps[cnt % 2]
            vs = []
            for n in range(8):
                v_ = vz[(cnt * 8 + n) % 16]
                em.ts("dve", v_.v(), RV.v((SL, n, slice(256 * h, 256 * h + 256)), keys=((n, h // 2),)),
                      wts.v((SL, d_, h, slice(n, n + 1))), None, ALU.mult)
                vs.append(v_)
            for n in range(8):
                em.mm(ps.v((SL, slice(0, 256))), ktok.v((SL, h, n), keys=((h, n),)), vs[n].v(), start=(n == 0), stop=(n == 7), inc=True)
            em.copy("act", lsb[cnt % 2].v(), ps.v((SL, slice(0, 256))))
            em.dma("sp", View(lfb, lfb.h[d_, h], ((d_, h),)), lsb[cnt % 2].v(), semof=lsb[cnt % 2].v())
            cnt += 1


def lay_w(w):
    n = w.shape[1] // 128
    return np.ascontiguousarray(w.reshape(KC, 128, n, 128).transpose(2, 1, 0, 3))


def rope_tables(pos):
    pos = np.asarray(pos)
    d = np.arange(128)
    a = d // 64
    f = d % 32
    p = (d % 64) // 32
    inv = 10000.0 ** (-(2.0 * f) / 64.0)
    rows = (pos // 64).astype(np.float64)
    cols = (pos % 64).astype(np.float64)
    coord = np.where(a[:, None] == 0, rows[None, :], cols[None, :])
    ang = (coord.astype(np.float32) * inv[:, None].astype(np.float32)).astype(np.float32)
    cos = np.cos(ang.astype(np.float64))
    sin = np.sin(ang.astype(np.float64)) * np.where(p == 0, -1.0, 1.0)[:, None]
    return cos.astype(np.float32), sin.astype(np.float32)


def perm_matrix():
    m = np.arange(128)
    partner = np.where((m % 64) < 32, m + 32, m - 32)
    P = np.zeros((128, 128), np.float32)
    P[partner, m] = 1.0
    return P


def inputs_B(core, hx_a, hc_a, mods0, mods1, inp):
    c = core
    S = hx_a.shape[0]
    lo, hi = LAT * c, LAT * (c + 1)
    hin = np.concatenate([hc_a.T, hx_a[lo:hi].T], axis=1)
    hl = hx_a[lo - 128:lo].T if c > 0 else np.zeros((D, 128), np.float32)
    hr = hx_a[hi:hi + 128].T if hi + 128 <= S else np.zeros((D, 128), np.float32)
    pos = np.concatenate([np.arange(lo, hi), np.arange(lo - 128, lo), np.arange(hi, hi + 128)])
    cos, sin = rope_tables(np.clip(pos, 0, 8191))
    kq = np.arange(128)
    mL = (kq[:, None] >= kq[None, :]).astype(np.float32)
    mR = (kq[:, None] <= kq[None, :]).astype(np.float32)
    z = np.zeros_like(mL)
    masks = np.stack([mL if c > 0 else z, mL, mR, mR if c < NCORE - 1 else z], axis=1)
    cw, cb = inp["ev_conv_w"][0], inp["ev_conv_b"][0]
    convp = np.stack([cw[0], cw[1], cw[2], cb], axis=-1).reshape(8, 128, 4).transpose(1, 0, 2)
    j = np.arange(128)[:, None]
    n = np.arange(8)[None, :]
    expo = np.stack([1023.0 - (128 * n + j), 128.0 * n + j], axis=1).astype(np.float32)
    m0 = np.stack([mods0[0][3:9], mods0[1][3:9]])
    m1 = np.stack([mods1[0][0:6], mods1[1][0:6]])
    wl1 = lay_w(inp["od_w_in"][0])
    return {
        "hin": np.ascontiguousarray(hin), "hhalo": np.ascontiguousarray(np.concatenate([hl, hr], axis=1)),
        "mod0": lay_vec(m0), "nw0": lay_vec(inp["norm_w"][0][1:3]),
        "mod1": lay_vec(m1), "nw1": lay_vec(inp["norm_w"][1][0:2]),
        "win": lay_w(inp["ev_w_in"][0]), "wout": lay_w(inp["ev_w_out"][0]),
        "wib": lay_wi(inp["ffn_b_wi"][0]), "wob": lay_wo(inp["ffn_b_wo"][0]),
        "wia": lay_wi(inp["ffn_a_wi"][1]), "woa": lay_wo(inp["ffn_a_wo"][1]),
        "cos": cos, "sin": sin, "pm": perm_matrix(), "masks": np.ascontiguousarray(masks),
        "sink": np.ascontiguousarray(inp["ev_sink"][0]), "convp": np.ascontiguousarray(convp),
        "edge": np.tile(np.array([[float(c > 0), float(c < NCORE - 1)]], np.float32), (128, 1)),
        "win1": np.ascontiguousarray(wl1[4:16]),
        "dec": np.stack([inp["od_decay_f"][0], inp["od_decay_b"][0]]).astype(np.float32),
        "expo": np.ascontiguousarray(expo),
    }


TTC = [(0, 256), (256, 512), (768, 512), (1280, 256), (1536, 256)]
TTL = [(0, 512), (512, 512)]
NA_PAIRS = [(j, s) for j in range(8) for s in ([-2, -1, 0, 1, 2] + ([3] if j == 0 else []) + ([-3] if j == 7 else []))]


def na_blk_col(b):
    if b < 0:
        return 1280 + 128 * (b + 2)
    if b < 8:
        return 256 + 128 * b
    return 1536 + 128 * (b - 8)


def build_C(upto=3):
    nc = bass.Bass("TRN2", target_bir_lowering=False)
    cx = Ctx(nc)
    em = cx.em
    din = lambda name, shape: nc.dram_tensor(name, list(shape), F32, kind="ExternalInput")
    hin = din("hin", [D, T0])
    hhalo = din("hhalo", [D, 512])
    mod1 = din("mod1", [128, 2, 6, KC])
    nw1 = din("nw1", [128, 2, KC])
    fnw = din("fnw", [128, KC])
    win = din("win", [48, 128, KC, 128])
    cosd, sind = din("cos", [128, LAT]), din("sin", [128, LAT])
    pmd = din("pm", [128, 128])
    decd = din("dec", [2, 4])
    lall = din("lall", [2, 8, 128, 4, 256])
    coefd = din("coef", [128, 2, 2, 9])
    rcd = din("rconst", [128, 6, 128])
    zcd = din("zconst", [128, 2, 3])
    gnwd = din("gnw", [128, 8])
    nab = din("nab", [len(NA_PAIRS), 2, 128, 4, 128]) if upto >= 1 else None
    yout = em.dram("yout", [D, LAT], F32, kind="ExternalOutput")
    norm_scratch(cx)
    make_ident(cx)
    XB = em.sbuf("XB", [128, KC, LAT], BF16)
    XB.tt = TTL
    m1, A1 = prep_mod(cx, mod1[:, :, :, :], nw1[:, :, :], 2)
    hin_v = hin.ap().rearrange("(k p) t -> p k t", p=128)
    hh_v = hhalo.ap().rearrange("(k p) t -> p k t", p=128)
    vec_of = [1, 0, 0, 0, 0]

    with em.scope():
        xm = em.sbuf("xm", [128, KC, 1792], BF16)
        xm.tt = TTC
        with em.scope():
            hst = [em.sbuf("hst%d" % i, [128, KC, 512], F32) for i in range(2)]
            for ti, (lo, n) in enumerate(TTC):
                hs = hst[ti % 2]
                hs.tt = [(0, n)]
                with em.dma_group("hst%d" % ti):
                    for k in range(KC):
                        src = hin_v[:, k, lo:lo + n] if ti < 3 else hh_v[:, k, lo - 1280:lo - 1280 + n]
                        em.dma("sp", kt(hs, k, 0), src)
                rmsnorm_mod(cx, hs, xm, [(0, ti, (lambda k, v=vec_of[ti]: A1.v((SL, v, 0, slice(k, k + 1)))),
                                          (lambda k, v=vec_of[ti]: m1.v((SL, v, 0, slice(k, k + 1)))))], 6)
        with em.scope():
            RQ = em.sbuf("RQ", [128, 4, LAT], BF16)
            RK = em.sbuf("RK", [128, 4, T0], BF16)
            RV = em.sbuf("RV", [128, 10, 1024], BF16)
            with em.scope():
                rope_setup(cx, cosd[:, :], sind[:, :], pmd[:, :], LAT)
                ring = Ring(em, "wringr", [KC, 128], BF16, 4)

                def evac_q(ci, xti, ps, n):
                    lo, _ = TTC[xti]
                    rope_evac(cx, ps, n, RQ.v((SL, ci, slice(lo - 256, lo - 256 + n)), keys=((ci, xti),)),
                              cx.COS.v((SL, slice(lo - 256, lo - 256 + n))), cx.SIN.v((SL, slice(lo - 256, lo - 256 + n))))
                proj_fm(cx, xm, [1, 2], [win[c] for c in range(4)], ring, evac_q)

                def evac_k(ci, xti, ps, n):
                    lo, _ = TTC[xti]
                    out = RK.v((SL, ci, slice(lo, lo + n)), keys=((ci, xti),))
                    if xti == 0:
                        em.copy("act", out, ps.v((SL, slice(0, n))))
                    else:
                        rope_evac(cx, ps, n, out, cx.COS.v((SL, slice(lo - 256, lo - 256 + n))), cx.SIN.v((SL, slice(lo - 256, lo - 256 + n))))
                proj_fm(cx, xm, [0, 1, 2], [win[4 + c] for c in range(4)], ring, evac_k)
                wt = em.sbuf("wt4", [128, 4, KC, 128], BF16)
                for half in range(2):
                    def evac_v(bi, ps, half=half):
                        em.copy("act", RV.v((SL, bi, slice(512 * half, 512 * half + 512)), keys=((bi, half),)), ps.v())
                    proj_tm(cx, xm, [128 * b for b in range(10)], [win[8 + 4 * half + c] for c in range(4)], wt, evac_v)
            retention_C(cx, XB, RQ, RK, RV, decd, lall, coefd, rcd, zcd, gnwd)
        if upto < 1:
            return _dump_C(cx, nc, XB, None, yout)
        with em.scope():
            ring = Ring(em, "wringg", [KC, 128], BF16, 4)
            sgt = [em.sbuf("sgt%d" % i, [128, 512], F32) for i in range(2)]
            gcnt = [0]

            def evac_g(ci, xti, ps, n):
                t_ = sgt[gcnt[0] % 2]
                gcnt[0] += 1
                em.act(t_.v(), ps.v(), AF.Silu)
                em.tt("dve", kt(XB, ci, xti - 1), kt(XB, ci, xti - 1), t_.v(), ALU.mult)
            proj_fm(cx, xm, [1, 2], [win[16 + c] for c in range(8)], ring, evac_g)
        with em.scope():
            NQ = em.sbuf("NQ", [128, 8, LAT], BF16)
            NK = em.sbuf("NK", [128, 8, 1792], BF16)
            NV = em.sbuf("NV", [128, 14, 1024], BF16)
            with em.scope():
                ring = Ring(em, "wringn", [KC, 128], BF16, 4)

                def evac_nq(ci, xti, ps, n):
                    lo, _ = TTC[xti]
                    em.copy("act", NQ.v((SL, ci, slice(lo - 256, lo - 256 + n)), keys=((ci, xti),)), ps.v((SL, slice(0, n))))
                proj_fm(cx, xm, [1, 2], [win[24 + c] for c in range(8)], ring, evac_nq)
                ecnt = [0]

                def evac_nk(ci, xti, ps, n):
                    lo, _ = TTC[xti]
                    eng = "act" if ecnt[0] % 2 == 0 else "dve"
                    ecnt[0] += 1
                    em.copy(eng, NK.v((SL, ci, slice(lo, lo + n)), keys=((ci, xti),)), ps.v((SL, slice(0, n))))
                proj_fm(cx, xm, [0, 1, 2, 3, 4], [win[32 + c] for c in range(8)], ring, evac_nk)
                wt = em.sbuf("wt4n", [128, 4, KC, 128], BF16)
                for half in range(2):
                    def evac_nv(bi, ps, half=half):
                        em.copy("act", NV.v((SL, bi, slice(512 * half, 512 * half + 512)), keys=((bi, half),)), ps.v())
                    proj_tm(cx, xm, [128 * b for b in range(14)], [win[40 + 4 * half + c] for c in range(4)], wt, evac_nv)
            with em.scope():
                attn_setup(cx)
                bring = Ring(em, "nabr", [4, 128], F32, 8)
                pair_idx = {p: i for i, p in enumerate(NA_PAIRS)}

                def tile_of(col):
                    for ti, (lo, n) in enumerate(TTC):
                        if lo <= col < lo + n:
                            return ti
                for j in range(8):
                    for hg in range(2):
                        rel = [s for (jj, s) in NA_PAIRS if jj == j]
                        slots = []
                        for s_ in rel:
                            b = j + s_
                            col = na_blk_col(b)
                            bs = bring.load(nab[pair_idx[(j, s_)], hg], eng="sp")
                            slots.append(dict(
                                k=(lambda hh, col=col: NK.v((SL, 4 * hg + hh, slice(col, col + 128)), keys=((4 * hg + hh, tile_of(col)),))),
                                v=(lambda hh, col=col: NV.v((SL, col // 128, slice(128 * (4 * hg + hh), 128 * (4 * hg + hh) + 128)), keys=((col // 128, hg),))),
                                bias=bring.view(bs)))
                        for cb in range(2):
                            col = 128 * cb
                            slots.append(dict(
                                k=(lambda hh, col=col: NK.v((SL, 4 * hg + hh, slice(col, col + 128)), keys=((4 * hg + hh, 0),))),
                                v=(lambda hh, col=col: NV.v((SL, col // 128, slice(128 * (4 * hg + hh), 128 * (4 * hg + hh) + 128)), keys=((col // 128, hg),)))))
                        xti = 1 if j < 4 else 2
                        qh = lambda hh: NQ.v((SL, 4 * hg + hh, slice(128 * j, 128 * j + 128)), keys=((4 * hg + hh, xti),))
                        out4 = XB.v((SL, slice(8 + 4 * hg, 12 + 4 * hg), slice(128 * j, 128 * j + 128)),
                                    keys=tuple((8 + 4 * hg + hh, j // 4) for hh in range(4)))
                        attn_tile(cx, None, qh, slots, ATT_SCALE, None, out4, False)

    if upto < 2:
        return _dump_C(cx, nc, XB, None, yout)
    wout = din("wout", [16, 128, KC, 128])
    H = em.sbuf("H", [128, KC, LAT], F32)
    H.tt = TTL
    with em.dma_group("Hld"):
        for k in range(KC):
            em.dma("sp", H.v((SL, k), keys=tuple((k, ti) for ti in range(2))), hin_v[:, k, 256:256 + LAT])
    with em.scope():
        ring = Ring(em, "wring2", [KC, 128], BF16, 4)

        def evac_o(ci, xti, ps, n):
            em.stt("dve", kt(H, ci, xti), ps.v((SL, slice(0, n))), m1.v((SL, 0, 2, slice(ci, ci + 1))), kt(H, ci, xti), ALU.mult, ALU.add)
        proj_fm(cx, XB, [0, 1], [wout[d] for d in range(16)], ring, evac_o)
    if upto < 3:
        return _dump_C(cx, nc, XB, H, yout)
    wib, wob = din("wib", [FC, 128, KC, 2, 128]), din("wob", [FC, 128, D])
    gh = em.sbuf("gh", [128, 2, KC], F32)
    em.ts("dve", gh.v(), m1.v((SL, SL, 5)), 0.5, None, ALU.mult)
    rmsnorm_mod(cx, H, XB, [(ti, ti, (lambda k: A1.v((SL, 0, 1, slice(k, k + 1)))),
                             (lambda k: m1.v((SL, 0, 3, slice(k, k + 1))))) for ti in range(2)], 6)
    ffn(cx, H, XB, [(ti, ti, 0) for ti in range(2)], wib, wob, lambda vec, d: gh.v((SL, vec, slice(d, d + 1))))
    fw_ = em.sbuf("fnw", [128, KC], F32)
    em.dma("sp", fw_.v(), fnw[:, :])
    yo_v = yout.h.ap().rearrange("(k p) t -> p k t", p=128)
    with em.scope():
        o32 = [em.sbuf("o32%d" % i, [128, 512], F32) for i in range(3)]
        oc = [0]

        def out_fn(k, ti, tmp_view, n):
            lo, _ = TTL[ti]
            o = o32[oc[0] % 3]
            oc[0] += 1
            em.act(o.v((SL, slice(0, n))), tmp_view, AF.Identity, scale=fw_.v((SL, slice(k, k + 1))))
            em.dma("sp", View(yout, yo_v[:, k, lo:lo + n], ((k, ti),)), o.v((SL, slice(0, n))), semof=o.v())
        rmsnorm_mod(cx, H, XB, [(ti, ti, None, None) for ti in range(2)], 6, out_fn=out_fn)
    em.finish([View(yout, yout.h[:, :], tuple((k, ti) for k in range(KC) for ti in range(2)))])
    return nc


def retention_C(cx, XB, RQ, RK, RV, decd, lall, coefd, rcd, zcd, gnwd):
    em = cx.em
    AX = mybir.AxisListType.X
    lg, lns = load_lg(cx, decd)
    rc = em.sbuf("rc", [128, 6, 128], F32)
    zc = em.sbuf("zc", [128, 2, 3], F32)
    gnw = em.sbuf("gnw", [128, 8], F32)
    cf = em.sbuf("cf", [128, 2, 2, 9], F32)
    em.dma("sp", rc.v(), rcd[:, :, :])
    em.dma("sp", zc.v(), zcd[:, :, :])
    em.dma("sp", gnw.v(), gnwd[:, :])
    em.dma("sp", cf.v(), coefd[:, :, :, :])
    MT = em.sbuf("MT", [128, 4, 128], F32)
    XIF = em.sbuf("XIF", [128, 4, 128], F32)
    XIB = em.sbuf("XIB", [128, 4, 128], F32)
    tmpm = em.sbuf("tmpm", [128, 128], F32)
    zeta = em.sbuf("zeta", [128, 2, 4], F32)
    cw = em.sbuf("cw", [128, 2, 4, 2], F32)
    gC = em.sbuf("gC", [128, 8], F32)
    coef = em.sbuf("coef", [128, 2, 4, 9], F32)
    lgc = lambda d_, h: lg.v((SL, slice(4 * d_ + h, 4 * d_ + h + 1)))
    for h in range(4):
        em.act(MT.v((SL, h)), rc.v((SL, 0)), AF.Exp, bias=lns.v(), scale=lgc(0, h))
        em.tt("dve", MT.v((SL, h)), MT.v((SL, h)), rc.v((SL, 1)), ALU.mult)
        em.act(tmpm.v(), rc.v((SL, 2)), AF.Exp, bias=lns.v(), scale=lgc(1, h))
        em.tt("dve", tmpm.v(), tmpm.v(), rc.v((SL, 3)), ALU.mult)
        em.tt("dve", MT.v((SL, h)), MT.v((SL, h)), tmpm.v(), ALU.add)
        em.act(XIF.v((SL, h)), rc.v((SL, 4)), AF.Exp, scale=lgc(0, h))
        em.act(XIB.v((SL, h)), rc.v((SL, 5)), AF.Exp, scale=lgc(1, h))
    for d_ in range(2):
        for h in range(4):
            em.act(zeta.v((SL, d_, slice(h, h + 1))), zc.v((SL, d_, slice(0, 1))), AF.Exp, bias=lns.v(), scale=lgc(d_, h))
            em.act(cw.v((SL, d_, h)), zc.v((SL, d_, slice(1, 3))), AF.Exp, bias=lns.v(), scale=lgc(d_, h))
            em.act(coef.v((SL, d_, h)), cf.v((SL, d_, 0)), AF.Exp, scale=lgc(d_, h))
            em.tt("dve", coef.v((SL, d_, h)), coef.v((SL, d_, h)), cf.v((SL, d_, 1)), ALU.mult)
    em.act(gC.v(), lg.v(), AF.Exp, scale=128.0)
    ktok = em.sbuf("ktok", [128, 4, 10, 128], BF16)
    for h in range(4):
        for b in range(10):
            pb = cx.ps[6 + (b % 2)]
            xti = 0 if b < 2 else (1 if b < 6 else 2)
            em.transpose(psbf(pb, 128), RK.v((SL, h, slice(128 * b, 128 * b + 128)), keys=((h, xti),)), cx.ident.v())
            em.copy("act", ktok.v((SL, h, b), keys=((h, b),)), psbf(pb, 128))
    vz = [em.sbuf("vz%d" % i, [128, 256], BF16) for i in range(8)]
    vzc = [0]

    def kv_mm(out_view, h, b, wcol):
        v_ = vz[vzc[0] % 8]
        vzc[0] += 1
        em.ts("dve", v_.v(), RV.v((SL, b, slice(256 * h, 256 * h + 256)), keys=((b, h // 2),)), wcol, None, ALU.mult)
        return v_

    S0 = em.sbuf("S0", [128, 2, 4, 256], F32)
    cnt = 0
    for d_ in range(2):
        for h in range(4):
            ps = cx.ps[cnt % 2]
            cnt += 1
            for b in range(2):
                v_ = kv_mm(None, h, b, cw.v((SL, d_, h, slice(b, b + 1))))
                em.mm(ps.v((SL, slice(0, 256))), ktok.v((SL, h, b), keys=((h, b),)), v_.v(), start=(b == 0), stop=(b == 1), inc=True)
            em.ts("dve", S0.v((SL, d_, h), keys=((d_, h),)), ps.v((SL, slice(0, 256))), coef.v((SL, d_, h, slice(8, 9))), None, ALU.mult)
    with em.scope():
        lr = Ring(em, "lring", [4, 256], F32, 3)
        for d_ in range(2):
            for r in range(8):
                s_ = lr.load(lall[d_, r], eng="sp")
                for h in range(4):
                    em.stt("dve", S0.v((SL, d_, h), keys=((d_, h),)), lr.view(s_, (h,)), coef.v((SL, d_, h, slice(r, r + 1))),
                           S0.v((SL, d_, h), keys=((d_, h),)), ALU.mult, ALU.add)
    SBst = em.sbuf("SBst", [128, 8, 4, 256], BF16)

    def state_update(d_, n):
        vs, ovs = [], []
        for h in range(4):
            vs.append(kv_mm(None, h, 2 + n, zeta.v((SL, d_, slice(h, h + 1)))))
        for h in range(4):
            bank = cx.ps[1] if h < 2 else cx.ps[7]
            ov = bank.v((SL, slice(256 * (h % 2), 256 * (h % 2) + 256)))
            em.mm(ov, ktok.v((SL, h, 2 + n), keys=((h, 2 + n),)), vs[h].v(), inc=True)
            ovs.append(ov)
        for h in range(4):
            em.stt("dve", S0.v((SL, d_, h), keys=((d_, h),)), S0.v((SL, d_, h), keys=((d_, h),)), gC.v((SL, slice(4 * d_ + h, 4 * d_ + h + 1))),
                   ovs[h], ALU.mult, ALU.add)

    for n in range(7, -1, -1):
        em.copy("act", SBst.v((SL, n), keys=(n,)), S0.v((SL, 1), keys=tuple((1, h) for h in range(4))))
        if n > 0:
            state_update(1, n)
    Sfb = [em.sbuf("Sfb%d" % i, [128, 4, 256], BF16) for i in range(2)]
    PT = [em.sbuf("PT%d" % i, [128, 4, 128], BF16) for i in range(2)]
    qf = [em.sbuf("qf%d" % i, [128, 4, 128], BF16) for i in range(2)]
    qb = [em.sbuf("qb%d" % i, [128, 4, 128], BF16) for i in range(2)]
    ysb = [em.sbuf("ysb%d" % i, [128, 4, 256], F32) for i in range(2)]
    sq = em.sbuf("sq", [128, 4, 256], F32)
    yn = [em.sbuf("yn%d" % i, [128, 4, 256], BF16) for i in range(2)]
    st = [em.sbuf("st%d" % i, [128, 5, 4], F32) for i in range(2)]

    def v3(t, a, b):
        return View(t, t.h[:].rearrange("p (a b) -> p a b", a=a), (None,))

    def head(n):
        i = n % 2
        xti = 1 if n < 4 else 2
        qcols = slice(128 * n, 128 * n + 128)
        em.copy("act", Sfb[i].v(), S0.v((SL, 0), keys=tuple((0, h) for h in range(4))))
        A = cx.ps[0]
        for h in range(4):
            em.mm(A.v((SL, slice(128 * h, 128 * h + 128))), RK.v((SL, h, slice(256 + 128 * n, 256 + 128 * n + 128)), keys=((h, xti),)),
                  RQ.v((SL, h, qcols), keys=((h, xti),)), inc=(h == 3))
        em.tt("dve", PT[i].v(), v3(A, 4, 128), MT.v(), ALU.mult)
        rq4 = RQ.v((SL, SL, qcols), keys=tuple((h, xti) for h in range(4)))
        em.tt("pool", qf[i].v(), rq4, XIF.v(), ALU.mult)
        em.tt("pool", qb[i].v(), rq4, XIB.v(), ALU.mult)
        yb = (cx.ps[2], cx.ps[3]) if i == 0 else (cx.ps[4], cx.ps[5])
        for h in range(4):
            yv = yb[h // 2].v((SL, slice(256 * (h % 2), 256 * (h % 2) + 256)))
            rvv = RV.v((SL, 2 + n, slice(256 * h, 256 * h + 256)), keys=((2 + n, h // 2),))
            em.mm(yv, View(PT[i], PT[i].h[:, h, :], (None,)), rvv, start=True, stop=False)
            em.mm(yv, View(qf[i], qf[i].h[:, h, :], (None,)), View(Sfb[i], Sfb[i].h[:, h, :], (None,)), start=False, stop=False)
            em.mm(yv, View(qb[i], qb[i].h[:, h, :], (None,)), SBst.v((SL, n, h), keys=(n,)), start=False, stop=True)

    def tail(n):
        i = n % 2
        qcols = slice(128 * n, 128 * n + 128)
        yb = (cx.ps[2], cx.ps[3]) if i == 0 else (cx.ps[4], cx.ps[5])
        y_ = ysb[i]
        for half in range(2):
            em.copy("act", View(y_, y_.h[:, 2 * half:2 * half + 2, :], (None,)), v3(yb[half], 2, 256))
        s_ = st[i]
        em.op("dve", lambda e, o=s_.h[:, 0, :], a=y_.h[:]: e.tensor_reduce(o, a, AX, ALU.add), [y_.v()], [s_.v()])
        em.tt("pool", sq.v(), y_.v(), y_.v(), ALU.mult)
        em.op("dve", lambda e, o=s_.h[:, 1, :], a=sq.h[:]: e.tensor_reduce(o, a, AX, ALU.add), [sq.v()], [s_.v()])
        em.ts("dve", s_.v((SL, 2)), s_.v((SL, 0)), 1.0 / 256, None, ALU.mult)
        em.tt("dve", s_.v((SL, 3)), s_.v((SL, 2)), s_.v((SL, 2)), ALU.mult)
        em.stt("dve", s_.v((SL, 3)), s_.v((SL, 1)), 1.0 / 256, s_.v((SL, 3)), ALU.mult, ALU.subtract)
        em.act(s_.v((SL, 4)), s_.v((SL, 3)), AF.Sqrt, bias=cx.epsc.v())
        em.recip(s_.v((SL, 4)), s_.v((SL, 4)))
        for h in range(4):
            em.ts("dve", View(yn[i], yn[i].h[:, h, :], (None,)), View(y_, y_.h[:, h, :], (None,)),
                  s_.v((SL, 2, slice(h, h + 1))), s_.v((SL, 4, slice(h, h + 1))), ALU.subtract, ALU.mult)
        trb = cx.ps[6]
        ynf = yn[i].h[:].rearrange("p a b -> p (a b)")
        for c in range(8):
            em.transpose(View(trb, trb.h.bitcast(BF16)[:, 128 * c:128 * c + 128], (None,)),
                         View(yn[i], ynf[:, 128 * c:128 * c + 128], (None,)), cx.ident.v(), inc=(c == 7))
        trv = View(trb, trb.h.bitcast(BF16)[:, 0:1024].rearrange("p (c q) -> p c q", c=8), (None,))
        gb = View(gnw, gnw.h[:, :].unsqueeze(2).broadcast_to([128, 8, 128]), (None,))
        em.tt("dve", XB.v((SL, slice(0, 8), qcols), keys=tuple((c, n // 4) for c in range(8))), trv, gb, ALU.mult)

    head(0)
    for n in range(8):
        if n < 7:
            state_update(0, n)
            head(n + 1)
        tail(n)


MCH = 18


def build_M():
    nc = bass.Bass("TRN2", target_bir_lowering=False)
    cx = Ctx(nc)
    em = cx.em
    adaw = nc.dram_tensor("adaw", [2 * MCH, 128, KC, 128], F32, kind="ExternalInput")
    adab = nc.dram_tensor("adab", [128, 2 * MCH], F32, kind="ExternalInput")
    cvec = nc.dram_tensor("cvec", [128, KC, 2], F32, kind="ExternalInput")
    mout = em.dram("mout", [128, 2 * MCH, 2], F32, kind="ExternalOutput")
    cv = em.sbuf("cv", [128, KC, 2], F32)
    cb = em.sbuf("cb", [128, KC, 2], BF16)
    ab = em.sbuf("ab", [128, 2 * MCH], F32)
    mo = em.sbuf("mo", [128, 2 * MCH, 2], F32)
    em.dma("sp", cv.v(), cvec[:, :, :])
    em.dma("sp", ab.v(), adab[:, :])
    em.act(cb.v(), cv.v(), AF.Silu)
    ring = Ring(em, "mring", [KC, 128], BF16, 4)
    for ci, ws in stream(ring, [adaw[i] for i in range(2 * MCH)], 3):
        ps = cx.ps[ci % 4]
        for k in range(KC):
            em.mm(ps.v((SL, slice(0, 2))), ring.view(ws, (k,)), cb.v((SL, k)), start=(k == 0), stop=(k == KC - 1))
        em.ts("dve", mo.v((SL, ci)), ps.v((SL, slice(0, 2))), ab.v((SL, slice(ci, ci + 1))), None, ALU.add)
    em.dma("sp", mout.v(), mo.v())
    em.finish([mout.v()])
    return nc


def run_M(inp):
    nc = build_M()
    aw, ab_ = inp["ada_w"], inp["ada_b"]
    cvec = np.ascontiguousarray(np.stack([lay_vec(inp["c"][0]), lay_vec(inp["c_ctx"])], axis=-1))
    maps = []
    ncol = MCH * 128
    for c in range(NCORE):
        wl = [lay_w(np.ascontiguousarray(aw[l][:, c * ncol:(c + 1) * ncol])) for l in range(2)]
        bl = [ab_[l][c * ncol:(c + 1) * ncol].reshape(MCH, 128).T for l in range(2)]
        maps.append({"adaw": np.ascontiguousarray(np.concatenate(wl, 0)), "adab": np.ascontiguousarray(np.concatenate(bl, 1)), "cvec": cvec})
    res = run_bass_kernel_spmd(nc, maps, core_ids=list(range(NCORE)))
    full = np.zeros((2, 2, 9 * D), np.float32)
    for c in range(NCORE):
        mo = res.results[c]["mout"]
        for l in range(2):
            blk = mo[:, l * MCH:(l + 1) * MCH, :]
            full[l, :, c * ncol:(c + 1) * ncol] = blk.transpose(2, 1, 0).reshape(2, ncol)
    return full.reshape(2, 2, 9, D)


def na_bias_tables(core, rpb):
    out = np.empty((len(NA_PAIRS), 2, 128, 4, 128), np.float32)
    loc = np.arange(128)
    lr, lc = loc // 64, loc % 64
    for pi, (j, s) in enumerate(NA_PAIRS):
        bq = 8 * core + j
        bk = bq + s
        r = (2 * bq + lr)[None, :]
        qc = lc[None, :]
        kr = (2 * bk + lr)[:, None]
        kc = lc[:, None]
        rs = np.clip(r - 4, 0, 120)
        cs = np.clip(qc - 8, 0, 48)
        valid = (kr >= rs) & (kr < rs + 8) & (kc >= cs) & (kc < cs + 16) & (bk >= 0) & (bk < 64)
        dr = np.clip(kr - r + 7, 0, 14)
        dc = np.clip(kc - qc + 15, 0, 30)
        b = rpb[:, dr, dc]
        b = np.where(valid[None], b, np.float32(-30000.0)).astype(np.float32)
        out[pi] = b.reshape(2, 4, 128, 128).transpose(0, 2, 1, 3)
    return out


def inputs_C(core, houts, lfbs, mods1, inp, wcache):
    c = core
    hin = houts[c]
    up = houts[c - 1][:, 256 + 768:256 + 1024] if c > 0 else np.zeros((D, 256), np.float32)
    dn = houts[c + 1][:, 256:512] if c < NCORE - 1 else np.zeros((D, 256), np.float32)
    cos, sin = rope_tables(np.arange(LAT * c, LAT * (c + 1)))
    lall = np.ascontiguousarray(np.stack(lfbs, axis=1).transpose(0, 1, 3, 2, 4))
    coef = np.zeros((2, 2, 9), np.float32)
    for r in range(8):
        if r < c:
            coef[0, 0, r], coef[0, 1, r] = 1024.0 * (c - 1 - r), 1.0
        if r > c:
            coef[1, 0, r], coef[1, 1, r] = 1024.0 * (r - c - 1), 1.0
    coef[0, 0, 8], coef[0, 1, 8] = 1024.0 * c, 1.0
    coef[1, 0, 8], coef[1, 1, 8] = 1024.0 * (7 - c), 1.0
    jj = np.arange(128, dtype=np.float32)
    J, I = jj[:, None], jj[None, :]
    rconst = np.stack([np.maximum(I - J, 0), (I >= J).astype(np.float32), np.maximum(J - I, 0), (J > I).astype(np.float32),
                       np.broadcast_to(I + 1, (128, 128)), np.broadcast_to(128 - I, (128, 128))], axis=1).astype(np.float32)
    zconst = np.stack([np.stack([127 - jj, 255 - jj, 127 - jj], -1), np.stack([jj, jj, 128 + jj], -1)], axis=1).astype(np.float32)
    m1 = np.stack([mods1[0][3:9], mods1[1][3:9]])
    d = {
        "hin": np.ascontiguousarray(hin), "hhalo": np.ascontiguousarray(np.concatenate([up, dn], axis=1)),
        "mod1": lay_vec(m1), "nw1": lay_vec(inp["norm_w"][1][1:3]), "fnw": lay_vec(inp["final_norm_w"]),
        "cos": cos, "sin": sin, "pm": perm_matrix(),
        "dec": np.stack([inp["od_decay_f"][0], inp["od_decay_b"][0]]).astype(np.float32),
        "lall": lall, "coef": np.ascontiguousarray(np.broadcast_to(coef[None], (128, 2, 2, 9))),
        "rconst": np.ascontiguousarray(rconst), "zconst": np.ascontiguousarray(zconst),
        "gnw": np.ascontiguousarray(inp["od_gn_w"][0].reshape(8, 128).T),
        "nab": na_bias_tables(c, inp["od_rpb"][0]),
    }
    d.update(wcache)
    return d


def kernel(**inp):
    inp = {k: np.asarray(v) for k, v in inp.items()}
    ids = list(range(NCORE))
    mods = run_M(inp)
    x, ctx = inp["x"][0], inp["ctx"][0]
    ncA = build_A()
    wA = {"wi": lay_wi(inp["ffn_a_wi"][0]), "wo": lay_wo(inp["ffn_a_wo"][0]), "mod": lay_vec(mods[0][:, 0:3]), "nw": lay_vec(inp["norm_w"][0][0:1])}
    mapsA = [dict(wA, xT=np.ascontiguousarray(np.concatenate([ctx[CO * c:CO * (c + 1)].T, x[LAT * c:LAT * (c + 1)].T], axis=1))) for c in ids]
    resA = run_bass_kernel_spmd(ncA, mapsA, core_ids=ids)
    hA = [resA.results[c]["hout"] for c in ids]
    del mapsA, wA
    hc_a = np.concatenate([hA[c][:, :CO].T for c in ids], axis=0)
    hx_a = np.concatenate([hA[c][:, CO:].T for c in ids], axis=0)
    ncB = build_B()
    base = inputs_B(0, hx_a, hc_a, (mods[0][0], mods[0][1]), (mods[1][0], mods[1][1]), inp)
    mapsB = [dict(base, **inputs_B_core(c, hx_a, hc_a)) for c in ids]
    resB = run_bass_kernel_spmd(ncB, mapsB, core_ids=ids)
    hB = [resB.results[c]["hout"] for c in ids]
    lfbs = [resB.results[c]["lfb"] for c in ids]
    hc_1a = np.concatenate([hB[c][:, CB:CTX].T for c in ids], axis=0)
    hB = [np.concatenate([hc_1a.T, hB[c][:, CTX:]], axis=1) for c in ids]
    del mapsB, base
    ncC = build_C()
    wC = {"win": lay_w(inp["od_w_in"][0]), "wout": lay_w(inp["od_w_out"][0]), "wib": lay_wi(inp["ffn_b_wi"][1]), "wob": lay_wo(inp["ffn_b_wo"][1])}
    mapsC = [inputs_C(c, hB, lfbs, (mods[1][0], mods[1][1]), inp, wC) for c in ids]
    resC = run_bass_kernel_spmd(ncC, mapsC, core_ids=ids)
    out = np.concatenate([resC.results[c]["yout"].T for c in ids], axis=0)
    return np.ascontiguousarray(out[None].astype(np.float32))


def inputs_B_core(core, hx_a, hc_a):
    c = core
    S = hx_a.shape[0]
    lo, hi = LAT * c, LAT * (c + 1)
    hin = np.concatenate([np.roll(hc_a, CB - CO * c, axis=0).T, hx_a[lo:hi].T], axis=1)
    hl = hx_a[lo - 128:lo].T if c > 0 else np.zeros((D, 128), np.float32)
    hr = hx_a[hi:hi + 128].T if hi + 128 <= S else np.zeros((D, 128), np.float32)
    pos = np.concatenate([np.arange(lo, hi), np.arange(lo - 128, lo), np.arange(hi, hi + 128)])
    cos, sin = rope_tables(np.clip(pos, 0, 8191))
    kq = np.arange(128)
    mL = (kq[:, None] >= kq[None, :]).astype(np.float32)
    mR = (kq[:, None] <= kq[None, :]).astype(np.float32)
    z = np.zeros_like(mL)
    masks = np.stack([mL if c > 0 else z, mL, mR, mR if c < NCORE - 1 else z], axis=1)
    return {"hin": np.ascontiguousarray(hin), "hhalo": np.ascontiguousarray(np.concatenate([hl, hr], axis=1)),
            "cos": cos, "sin": sin, "masks": np.ascontiguousarray(masks),
            "edge": np.tile(np.array([[float(c > 0), float(c < NCORE - 1)]], np.float32), (128, 1))}


def _dump_C(cx, nc, XB, H, yout):
    em = cx.em
    yo_v = yout.h.ap().rearrange("(k p) t -> p k t", p=128)
    with em.scope():
        tmp = [em.sbuf("dbgc%d" % i, [128, LAT], F32) for i in range(2)]
        for k in range(KC):
            t_ = tmp[k % 2]
            src = H.v((SL, k), keys=tuple((k, ti) for ti in range(2))) if H is not None else XB.v((SL, k), keys=tuple((k, ti) for ti in range(2)))
            em.copy("dve", t_.v(), src)
            em.dma("sp", View(yout, yo_v[:, k, :], tuple((k, ti) for ti in range(2))), t_.v(), semof=t_.v())
        em.finish([View(yout, yout.h[:, :], tuple((k, ti) for k in range(KC) for ti in range(2)))], close=False)
    return nc
```
